# Optimizing a Trainium2 kernel written in Bass

```python
import math
import jax, jax.numpy as jnp
from jax import lax
import numpy as np

D_MODEL = 4096
BATCH = 2
SEQ = 8192
DEPTH = 2

HEAD_DIM = 128
RET_HEADS = D_MODEL // (2 * HEAD_DIM)
ATT_HEADS = D_MODEL // (2 * HEAD_DIM)
RET_WIDTH = RET_HEADS * HEAD_DIM
ATT_WIDTH = ATT_HEADS * HEAD_DIM
AB_IN = 4 * RET_WIDTH + 3 * ATT_WIDTH
AB_OUT = RET_WIDTH + ATT_WIDTH
AB_SPLITS = [RET_WIDTH, 2 * RET_WIDTH, 3 * RET_WIDTH, 4 * RET_WIDTH,
             4 * RET_WIDTH + ATT_WIDTH, 4 * RET_WIDTH + 2 * ATT_WIDTH]
RET_CHUNK = 128
ROPE_BASE = 10000.0
SWA_PATTERNS = ((128, 1), (512, 4), (2048, 16))
SWA_BLOCK = 128
DN_HEADS = D_MODEL // HEAD_DIM
DN_WIDTH = DN_HEADS * HEAD_DIM
DN_IN = 4 * DN_WIDTH + 2 * DN_HEADS
DN_SPLITS = [3 * DN_WIDTH, 4 * DN_WIDTH, 4 * DN_WIDTH + DN_HEADS]
DN_CONV = 4
DN_CHUNK = 64
D_FF = 256 * (-(-(8 * D_MODEL) // (3 * 256)))
LN_EPS = 1e-5
NORM_EPS = 1e-6
DEEPNORM_ALPHA = (2.0 * DEPTH) ** 0.25
DEEPNORM_BETA = (8.0 * DEPTH) ** -0.25
N_EVEN = (DEPTH + 1) // 2
N_ODD = DEPTH // 2

kernel_name = "hybrid_retention_dilatedswa_gdn_macaron_deepnorm"


def _layer_norm(x, w, b):
    x32 = x.astype(jnp.float32)
    mu = x32.mean(-1, keepdims=True)
    var = jnp.square(x32 - mu).mean(-1, keepdims=True)
    return ((x32 - mu) * lax.rsqrt(var + LN_EPS) * w + b).astype(x.dtype)


def _swiglu(x, w_gate, w_up, w_down):
    return (jax.nn.silu(x @ w_gate) * (x @ w_up)) @ w_down


def _heads(t, n):
    b, s, _ = t.shape
    return t.reshape(b, s, n, HEAD_DIM).transpose(0, 2, 1, 3)


def _merge(t):
    b, n, s, d = t.shape
    return t.transpose(0, 2, 1, 3).reshape(b, s, n * d)


def _rotate(x, cos, sin):
    x1, x2 = x[..., ::2], x[..., 1::2]
    return jnp.stack([x1 * cos - x2 * sin, x1 * sin + x2 * cos], axis=-1).reshape(x.shape)


def _retention(q, k, v):
    b, h, s, d = q.shape
    c = RET_CHUNK
    n = s // c
    log_g = jnp.log1p(-jnp.exp2(-5.0 - jnp.arange(h, dtype=jnp.float32)))
    idx = jnp.arange(c, dtype=jnp.float32)
    rel = idx[:, None] - idx[None, :]
    dmask = jnp.where(rel >= 0, jnp.exp(log_g[:, None, None] * jnp.maximum(rel, 0.0)), 0.0)
    qc, kc, vc = (t.reshape(b, h, n, c, d) for t in (q, k, v))
    scores = jnp.einsum("bhnid,bhnjd->bhnij", qc, kc) * dmask[:, None]
    o_intra = jnp.einsum("bhnij,bhnje->bhnie", scores, vc)
    zeta = jnp.exp(log_g[:, None] * (c - 1 - idx))
    xi = jnp.exp(log_g[:, None] * (idx + 1))
    kv = jnp.einsum("bhnjd,bhnje->bhnde", kc * zeta[:, None, :, None], vc)
    chunk_decay = jnp.exp(log_g * c)[:, None, None]

    def step(state, kv_n):
        return state * chunk_decay + kv_n, state

    _, prev = lax.scan(step, jnp.zeros((b, h, d, d), jnp.float32), jnp.moveaxis(kv, 2, 0))
    prev = jnp.moveaxis(prev, 0, 2)
    o_cross = jnp.einsum("bhnid,bhnde->bhnie", qc, prev) * xi[:, None, :, None]
    return (o_intra + o_cross).reshape(b, h, s, d)


def _dilated_window_attention(q, k, v, window, dilation):
    b, h, s, d = q.shape
    n_keys = window // dilation
    blk = SWA_BLOCK
    span = blk * dilation
    padded = -(-s // span) * span
    length = padded // dilation
    nb = length // blk

    def by_residue(t):
        t = jnp.pad(t, ((0, 0), (0, 0), (0, padded - s), (0, 0)))
        t = t.reshape(b, h, length, dilation, d).transpose(0, 1, 3, 2, 4)
        return t.reshape(b, h, dilation, nb, blk, d)

    def with_prev(t):
        prev = jnp.concatenate([jnp.zeros_like(t[:, :, :, :1]), t[:, :, :, :-1]], axis=3)
        return jnp.concatenate([prev, t], axis=4)

    qb = by_residue(q)
    kk = with_prev(by_residue(k))
    vv = with_prev(by_residue(v))
    scores = jnp.einsum("bhrnid,bhrnjd->bhrnij", qb, kk).astype(jnp.float32) * (HEAD_DIM ** -0.5)
    i = jnp.arange(blk)[:, None] + blk
    j = jnp.arange(2 * blk)[None, :]
    dist = i - j
    valid = (dist >= 0) & (dist <= n_keys)
    first = (jnp.arange(nb) == 0)[:, None, None] & (j < blk)[None]
    valid = valid[None] & ~first
    scores = jnp.where(valid, scores, -jnp.inf)
    lse = jax.nn.logsumexp(scores, axis=-1)
    p = jnp.exp(scores - lse[..., None])
    o = jnp.einsum("bhrnij,bhrnjd->bhrnid", p, vv.astype(jnp.float32))

    def back(t):
        rest = t.shape[5:]
        t = t.reshape((b, h, dilation, length) + rest)
        t = jnp.moveaxis(t, 2, 3).reshape((b, h, padded) + rest)
        return t[:, :, :s]

    return back(o), back(lse)


def _mixer_retention_swa(x, w_in, gn_w, w_out):
    b, s, _ = x.shape
    f32 = jnp.float32
    proj = x @ w_in
    rq, rk, rv, rg, aq, ak, av = jnp.split(proj, AB_SPLITS, axis=-1)
    pos = jnp.arange(s, dtype=f32)
    inv_freq = 1.0 / (ROPE_BASE ** jnp.linspace(0.0, 1.0, HEAD_DIM // 2, dtype=f32))
    ang = pos[:, None] * inv_freq[None, :]
    cos, sin = jnp.cos(ang), jnp.sin(ang)
    q = _rotate(_heads(rq, RET_HEADS).astype(f32), cos, sin)
    k = _rotate(_heads(rk, RET_HEADS).astype(f32), cos, sin) * (HEAD_DIM ** -0.5)
    v = _heads(rv, RET_HEADS).astype(f32)
    y = _retention(q, k, v)
    mu = y.mean(-1, keepdims=True)
    var = jnp.square(y - mu).mean(-1, keepdims=True)
    y = (y - mu) * lax.rsqrt(var + LN_EPS) * gn_w.astype(f32).reshape(RET_HEADS, 1, HEAD_DIM)
    y_ret = _merge(y).astype(x.dtype) * jax.nn.silu(rg)
    q = _heads(aq, ATT_HEADS)
    k = _heads(ak, ATT_HEADS)
    v = _heads(av, ATT_HEADS)
    outs, lses = [], []
    for window, dilation in SWA_PATTERNS:
        o, lse = _dilated_window_attention(q, k, v, window, dilation)
        outs.append(o)
        lses.append(lse)
    wts = jax.nn.softmax(jnp.stack(lses), axis=0)
    y_att = jnp.sum(wts[..., None] * jnp.stack(outs), axis=0)
    y_att = _merge(y_att).astype(x.dtype)
    return jnp.concatenate([y_ret, y_att], axis=-1) @ w_out


def _causal_conv(x, w):
    kw = w.shape[0]
    s = x.shape[1]
    xp = jnp.pad(x, ((0, 0), (kw - 1, 0), (0, 0)))
    out = xp[:, 0:s] * w[0]
    for i in range(1, kw):
        out = out + xp[:, i:i + s] * w[i]
    return out


def _l2norm(t):
    return t * lax.rsqrt(jnp.sum(t * t, axis=-1, keepdims=True) + NORM_EPS)


def _gated_delta_rule(q, k, v, g, beta):
    b, h, s, d = q.shape
    c = DN_CHUNK
    n = s // c
    f32 = jnp.float32
    q, k, v = (t.reshape(b, h, n, c, d) for t in (q, k, v))
    g = jnp.cumsum(g.reshape(b, h, n, c), axis=-1)
    beta = beta.reshape(b, h, n, c)
    idx = jnp.arange(c)
    lower = idx[:, None] >= idx[None, :]
    strict = idx[:, None] > idx[None, :]
    decay = jnp.exp(jnp.where(lower, g[..., :, None] - g[..., None, :], -jnp.inf))
    k_beta = k * beta[..., None]
    a_mat = jnp.where(strict, jnp.einsum("bhnid,bhnjd->bhnij", k_beta, k) * decay, 0.0)
    eye = jnp.eye(c, dtype=f32)
    t_mat = lax.linalg.triangular_solve(a_mat + eye, jnp.broadcast_to(eye, a_mat.shape),
                                        left_side=True, lower=True, unit_diagonal=True)
    u = t_mat @ (v * beta[..., None])
    w = t_mat @ (k_beta * jnp.exp(g)[..., None])
    qk = jnp.einsum("bhnid,bhnjd->bhnij", q, k) * decay
    q_dec = q * jnp.exp(g)[..., None]
    g_last = g[..., -1:]
    k_dec = k * jnp.exp(g_last - g)[..., None]
    last = jnp.exp(g_last)[..., None]

    def step(state, inp):
        u_n, w_n, qd_n, kd_n, qk_n, last_n = inp
        v_new = u_n - w_n @ state
        o_n = qd_n @ state + qk_n @ v_new
        state = state * last_n + jnp.swapaxes(kd_n, -1, -2) @ v_new
        return state, o_n

    xs = tuple(jnp.moveaxis(t, 2, 0) for t in (u, w, q_dec, k_dec, qk, last))
    _, o = lax.scan(step, jnp.zeros((b, h, d, d), f32), xs)
    return jnp.moveaxis(o, 0, 2).reshape(b, h, s, d)


def _mixer_gated_deltanet(x, w_in, conv_w, a_log, dt_bias, norm_w, w_out):
    f32 = jnp.float32
    proj = x @ w_in
    qkv, gate, a, beta_logit = jnp.split(proj, DN_SPLITS, axis=-1)
    qkv = jax.nn.silu(_causal_conv(qkv, conv_w))
    q, k, v = jnp.split(qkv, 3, axis=-1)
    q = _l2norm(_heads(q, DN_HEADS).astype(f32)) * (HEAD_DIM ** -0.5)
    k = _l2norm(_heads(k, DN_HEADS).astype(f32))
    v = _heads(v, DN_HEADS).astype(f32)
    beta = jax.nn.sigmoid(beta_logit.astype(f32)).transpose(0, 2, 1)
    g = (-jnp.exp(a_log.astype(f32)) * jax.nn.softplus(a.astype(f32) + dt_bias.astype(f32))).transpose(0, 2, 1)
    o = _gated_delta_rule(q, k, v, g, beta)
    o = o * lax.rsqrt(jnp.mean(o * o, axis=-1, keepdims=True) + NORM_EPS) * norm_w.astype(f32)
    o = _merge(o).astype(x.dtype) * jax.nn.silu(gate)
    return o @ w_out


def setup_inputs(seed: int = 0) -> dict:
    key = jax.random.key(seed)
    ks = jax.random.split(key, 16)
    f32 = jnp.float32

    def normal(k, shape, scale):
        return jax.random.normal(k, shape, f32) * scale

    x = normal(ks[0], (BATCH, SEQ, D_MODEL), 1.0)
    ffn_w_gate = normal(ks[1], (DEPTH, 2, D_MODEL, D_FF), D_MODEL ** -0.5)
    ffn_w_up = normal(ks[2], (DEPTH, 2, D_MODEL, D_FF), D_MODEL ** -0.5 * DEEPNORM_BETA)
    ffn_w_down = normal(ks[3], (DEPTH, 2, D_FF, D_MODEL), D_FF ** -0.5 * DEEPNORM_BETA)
    ln_w = 1.0 + normal(ks[4], (DEPTH, 3, D_MODEL), 0.02)
    ln_b = normal(ks[5], (DEPTH, 3, D_MODEL), 0.02)
    ab_scale = jnp.concatenate([
        jnp.ones((2 * RET_WIDTH,), f32), jnp.full((RET_WIDTH,), DEEPNORM_BETA, f32),
        jnp.ones((RET_WIDTH + 2 * ATT_WIDTH,), f32), jnp.full((ATT_WIDTH,), DEEPNORM_BETA, f32)])
    ab_w_in = normal(ks[6], (N_EVEN, D_MODEL, AB_IN), D_MODEL ** -0.5) * ab_scale
    ab_gn_w = 1.0 + normal(ks[7], (N_EVEN, RET_WIDTH), 0.02)
    ab_w_out = normal(ks[8], (N_EVEN, AB_OUT, D_MODEL), AB_OUT ** -0.5 * DEEPNORM_BETA)
    dn_scale = jnp.concatenate([
        jnp.ones((2 * DN_WIDTH,), f32), jnp.full((DN_WIDTH,), DEEPNORM_BETA, f32),
        jnp.ones((DN_WIDTH + 2 * DN_HEADS,), f32)])
    dn_w_in = normal(ks[9], (N_ODD, D_MODEL, DN_IN), D_MODEL ** -0.5) * dn_scale
    dn_conv_w = normal(ks[10], (N_ODD, DN_CONV, 3 * DN_WIDTH), DN_CONV ** -0.5)
    dn_a_log = jnp.log(jax.random.uniform(ks[11], (N_ODD, DN_HEADS), f32, 1.0, 16.0))
    dt = jnp.exp(jax.random.uniform(ks[12], (N_ODD, DN_HEADS), f32, math.log(1e-3), math.log(1e-1)))
    dn_dt_bias = dt + jnp.log(-jnp.expm1(-dt))
    dn_norm_w = 1.0 + normal(ks[13], (N_ODD, HEAD_DIM), 0.02)
    dn_w_out = normal(ks[14], (N_ODD, DN_WIDTH, D_MODEL), DN_WIDTH ** -0.5 * DEEPNORM_BETA)
    return {"x": x, "ffn_w_gate": ffn_w_gate, "ffn_w_up": ffn_w_up, "ffn_w_down": ffn_w_down,
            "ln_w": ln_w, "ln_b": ln_b, "ab_w_in": ab_w_in, "ab_gn_w": ab_gn_w, "ab_w_out": ab_w_out,
            "dn_w_in": dn_w_in, "dn_conv_w": dn_conv_w, "dn_a_log": dn_a_log, "dn_dt_bias": dn_dt_bias,
            "dn_norm_w": dn_norm_w, "dn_w_out": dn_w_out}


def reference(x, ffn_w_gate, ffn_w_up, ffn_w_down, ln_w, ln_b, ab_w_in, ab_gn_w, ab_w_out,
              dn_w_in, dn_conv_w, dn_a_log, dn_dt_bias, dn_norm_w, dn_w_out):
    for l in range(DEPTH):
        h = _swiglu(x, ffn_w_gate[l, 0], ffn_w_up[l, 0], ffn_w_down[l, 0])
        x = _layer_norm(DEEPNORM_ALPHA * x + 0.5 * h, ln_w[l, 0], ln_b[l, 0])
        if l % 2 == 0:
            m = _mixer_retention_swa(x, ab_w_in[l // 2], ab_gn_w[l // 2], ab_w_out[l // 2])
        else:
            i = l // 2
            m = _mixer_gated_deltanet(x, dn_w_in[i], dn_conv_w[i], dn_a_log[i], dn_dt_bias[i],
                                      dn_norm_w[i], dn_w_out[i])
        x = _layer_norm(DEEPNORM_ALPHA * x + m, ln_w[l, 1], ln_b[l, 1])
        h = _swiglu(x, ffn_w_gate[l, 1], ffn_w_up[l, 1], ffn_w_down[l, 1])
        x = _layer_norm(DEEPNORM_ALPHA * x + 0.5 * h, ln_w[l, 2], ln_b[l, 2])
    return x
```

```python
import numpy as np
from contextlib import ExitStack
import concourse.bass as bass
import concourse.mybir as mybir

F32, BF16 = mybir.dt.float32, mybir.dt.bfloat16
AF = mybir.ActivationFunctionType
ALU = mybir.AluOpType


class Res:
    __slots__ = ("name", "w", "rd", "rd_dma", "sem", "cnt", "alias", "keep", "qt")

    def __init__(self, name):
        self.name = name
        self.w = None
        self.rd = {}
        self.rd_dma = []
        self.sem = None
        self.cnt = 0
        self.alias = []
        self.keep = False
        self.qt = None


def alias(a, b):
    a.alias.append(b)
    b.alias.append(a)


class Op:
    __slots__ = ("eng", "fn", "deps", "dma", "sig", "val", "idx", "cc")


class Sched:
    def __init__(self, nc, es):
        self.nc = nc
        self.es = es
        self.ops = []
        self.engs = {"pe": nc.tensor, "act": nc.scalar, "dve": nc.vector, "pool": nc.gpsimd, "sp": nc.sync}
        self.esem = {}
        self.nsem = 0
        self.allres = []
        self.cnt = {}
        self.seen = {e: {} for e in self.engs}
        self.last = {}
        self.dmas = []
        self.nwait = 0
        self.nops = 0
        self.nid = 0
        self.free_sems = {'sw': [], 'hw': []}

    def res(self, name):
        r = Res(name)
        self.allres.append(r)
        return r

    def barrier_all(self):
        pend = [o for o in self.last.values()] + list(self.dmas)
        for e in self.engs:
            o = Op()
            o.cc, o.eng, o.fn, o.dma, o.sig, o.val = False, e, None, None, False, None
            self.nid += 1
            o.idx = self.nid
            o.deps = list(pend)
            for d in pend:
                d.sig = True
            self.ops.append(o)
        for r in self.allres:
            r.w, r.rd, r.rd_dma = None, {}, []
        self.last = {}
        self.dmas = []

    def op(self, eng, fn, rd=(), wr=(), dma=None, cc=False):
        o = Op()
        o.cc = cc
        o.eng, o.fn, o.dma, o.sig, o.val = eng, fn, dma, False, None
        self.nid += 1
        o.idx = self.nid
        deps = {}
        rds, wrs = [], []
        for r in rd:
            rds.append(r)
            rds.extend(r.alias)
        for w in wr:
            wrs.append(w)
            wrs.extend(w.alias)
        for r in rds:
            if r.w is not None:
                self._dep(o, r.w, deps, raw=True)
        for w in wrs:
            if w.w is not None:
                self._dep(o, w.w, deps, raw=False)
            for x in w.rd.values():
                self._dep(o, x, deps, raw=False)
            for x in w.rd_dma:
                self._dep(o, x, deps, raw=False)
        o.deps = list(deps.values())
        for d in o.deps:
            d.sig = True
        for r in rds:
            if o.dma is not None:
                r.rd_dma.append(o)
            else:
                r.rd[eng] = o
        for w in wrs:
            w.w = o
            w.rd = {}
            w.rd_dma = []
        self.ops.append(o)
        if fn is not None:
            if o.dma is not None:
                self.dmas.append(o)
            else:
                self.last[eng] = o
        return o

    def _dep(self, o, d, deps, raw):
        if d is o:
            return
        if d.dma is None and o.dma is None and d.eng == o.eng:
            if o.eng == "pe":
                return
        deps[d.idx] = d

    def barrier_wait(self, eng, reslist):
        return self.op(eng, None, rd=reslist)

    def _newsem(self, name):
        self.nsem += 1
        return self.es.enter_context(self.nc.semaphore(f"{name}_{self.nsem}"))

    def flush(self):
        nc = self.nc
        cnt, seen = self.cnt, self.seen
        if not self.esem:
            for e in self.engs:
                self.esem[e] = self._newsem("e_" + e)
                cnt[e] = 0
        nwait = 0
        used = []
        for o in self.ops:
            E = self.engs[o.eng]
            sn = seen[o.eng]
            for d in o.deps:
                if d.dma is not None:
                    sem = d.dma.sem
                    val = d.val if d.cc else 16 * d.dma.cnt
                else:
                    sem = self.esem[d.eng]
                    val = d.val
                k = id(sem)
                if sn.get(k, 0) >= val:
                    continue
                sn[k] = val
                E.wait_ge(sem, val)
                nwait += 1
            if o.fn is None:
                continue
            ins = o.fn(E)
            if o.dma is not None:
                r = o.dma
                qt = "cc" if o.cc else ("sw" if o.eng == "pool" else "hw")
                assert r.qt in (None, qt), (r.name, r.qt, qt)
                r.qt = qt
                if r.sem is None:
                    if qt != "cc" and self.free_sems[qt]:
                        r.sem, r.cnt = self.free_sems[qt].pop()
                    else:
                        r.sem = self._newsem("d_" + r.name)
                    used.append(r)
                r.cnt += 1
                if o.cc:
                    o.val = r.cnt
                    ins.then_inc(r.sem)
                else:
                    ins.then_inc(r.sem, 16)
            elif o.sig:
                cnt[o.eng] += 1
                o.val = cnt[o.eng]
                ins.then_inc(self.esem[o.eng], 1)
        for r in used:
            if not r.keep and r.qt != "cc":
                self.free_sems[r.qt].append((r.sem, r.cnt))
                r.sem = None
        self.allres = [r for r in self.allres if r.keep]
        self.nwait += nwait
        self.nops += len(self.ops)
        self.stats = dict(n_ops=self.nops, n_wait=self.nwait, n_sem=self.nsem, sig=dict(cnt))
        self.ops = []


class Gather:
    def __init__(self, S, nc, name, ext, nsc, rpr, cols, ccres, nr=4, bufs=None):
        self.S, self.nc = S, nc
        self.nsc, self.rpr, self.cols, self.nr = nsc, rpr, cols, nr
        if bufs is None:
            bufs = (nc.dram_tensor(name + "_b", [nsc * rpr, cols], F32), nc.dram_tensor(name + "_g", [nsc * nr * rpr, cols], F32))
        self.b, self.g = bufs
        self.B = S.res(name + "_B")
        self.G = [S.res(f"{name}_G{i}") for i in range(nsc)]
        self.cc = ccres
        self.done = 0
        bb = self.b.ap()
        S.op("sp", lambda E: E.dma_start(out=bb, in_=ext), wr=[self.B], dma=self.B)

    def ensure(self, sc):
        sc = min(sc, self.nsc - 1)
        while self.done <= sc:
            i = self.done
            src = self.b.ap()[i * self.rpr:(i + 1) * self.rpr, :]
            dst = self.g.ap()[i * self.nr * self.rpr:(i + 1) * self.nr * self.rpr, :]
            self.S.op("pool", lambda E, src=src, dst=dst: E.collective_compute(
                "AllGather", ALU.bypass, replica_groups=[[0, 1, 2, 3], [4, 5, 6, 7]], ins=[src.opt()], outs=[dst.opt()]),
                rd=[self.B], wr=[self.G[i]], dma=self.cc, cc=True)
            self.done += 1

    def rows(self, sc, r0, r1):
        base = sc * self.nr * self.rpr
        return self.g.ap()[base + r0:base + r1, :]

KC = 32
TT = 512
LN_EPS = 1e-5
ALPHA = 4.0 ** 0.25


class WStream:
    def __init__(self, S, nc, es, nslot, name="wr"):
        self.S = S
        self.n = nslot
        self.buf = es.enter_context(nc.sbuf_tensor(name, [128, nslot, 4096], BF16))
        self.res = [S.res(f"{name}{i}") for i in range(nslot)]
        self.pending = []
        self.issued = 0
        self.consumed = 0

    def push(self, src_ap, ncols, gath=None, scs=(), la=4):
        self.pending.append((src_ap, ncols, gath, scs, la))

    def _issue(self, i):
        src, ncols, gath, scs, la = self.pending[i]
        s = i % self.n
        dst = self.buf[:, s, 0:ncols]
        rd = []
        if gath is not None:
            gath.ensure(max(scs) + la)
            rd = [gath.G[k] for k in scs]
        self.S.op("pool", lambda E, dst=dst, src=src: E.dma_start(out=dst, in_=src),
                  rd=rd, wr=[self.res[s]], dma=self.res[s])

    def next(self):
        j = self.consumed
        while self.issued < min(len(self.pending), j + self.n):
            self._issue(self.issued)
            self.issued += 1
        self.consumed += 1
        s = j % self.n
        return self.buf[:, s, :], self.res[s]


def ffn_alloc(S, nc, es, NFC, sfx=""):
    A = {}
    _n = nc.sbuf_tensor
    A["U1"] = es.enter_context(nc.sbuf_tensor("U1" + sfx, [128, KC * TT], F32))
    A["HT"] = es.enter_context(nc.sbuf_tensor("HT" + sfx, [128, NFC * TT], BF16))
    A["x32"] = es.enter_context(nc.sbuf_tensor("x32" + sfx, [128, 2, TT], F32))
    A["ones"] = es.enter_context(nc.sbuf_tensor("ones" + sfx, [128, 128], F32))
    A["lnw"] = es.enter_context(nc.sbuf_tensor("lnw_sb" + sfx, [128, KC], F32))
    A["lnb"] = es.enter_context(nc.sbuf_tensor("lnb_sb" + sfx, [128, KC], F32))
    if NFC * TT * 2 < 10 * TT * 4:
        A["LTb"] = es.enter_context(nc.sbuf_tensor("LTb" + sfx, [128, 10, TT], F32))
    A["ps"] = [es.enter_context(nc.psum_tensor(f"ps{i}" + sfx, [128, TT], F32)) for i in range(8)]
    A["psr"] = [S.res(f"ps{i}") for i in range(8)]
    return A


def rl_all(X, t0):
    hf, o = t0 // 1024, t0 % 1024
    return X[:, hf, :, o:o + TT].rearrange("c p t -> p c t")


def rl_one(X, dc, t0):
    hf, o = t0 // 1024, t0 % 1024
    return X[dc, hf, :, o:o + TT]


def emit_ffn_ln(S, nc, A, ws, x_dram, y_dram, wg, wu, wd, lnw, lnb, NT, NFC, first=True):
    U1 = A["U1"]
    r = U1[:].rearrange("p (c t) -> p c t", c=KC)
    U1b = U1.bitcast(BF16)
    xb = U1b[:, 0:KC * TT].rearrange("p (c t) -> p c t", c=KC)
    sil = [r[:, 16, :], r[:, 17, :]]
    HT2 = A["HT"]
    HT = HT2[:].rearrange("p (f t) -> p f t", f=NFC)
    ps, psr = A["ps"], A["psr"]
    ones = A["ones"]
    XB = S.res("XB")
    R = [S.res(f"R{d}") for d in range(KC)]
    for d in range(16):
        alias(R[d], XB)
    H = [S.res(f"H{f}") for f in range(NFC)]
    X32 = [S.res("x32a"), S.res("x32b")]
    x32 = A["x32"]
    LT = [S.res(f"LT{k}") for k in range(10)]
    if "LTb" in A:
        lt = [A["LTb"][:, k, :] for k in range(10)]
    else:
        HTf = HT2.bitcast(F32)
        lt = [HTf[:, k * TT:(k + 1) * TT] for k in range(10)]
        for k in range(10):
            alias(LT[k], H[2 * k])
            alias(LT[k], H[2 * k + 1])
    CONST = S.res("const")
    if first:
        S.op("dve", lambda E: E.memset(ones[:], 1.0), wr=[CONST])
    LNP = S.res("lnp")
    S.op("sp", lambda E: E.dma_start(out=A["lnw"][:], in_=lnw), wr=[LNP], dma=LNP)
    S.op("sp", lambda E: E.dma_start(out=A["lnb"][:], in_=lnb), wr=[LNP], dma=LNP)

    npc = -(-NFC // 32)
    bnd = [round(i * NFC / npc) for i in range(npc + 1)]
    thirds = [(bnd[i], bnd[i + 1]) for i in range(npc)]
    for ti in range(NT):
        for fc in range(NFC):
            ws.push(wg.rows(fc // 2, (fc % 2) * 128, (fc % 2 + 1) * 128), 4096, wg, (fc // 2,))
            ws.push(wu.rows(fc // 2, (fc % 2) * 128, (fc % 2 + 1) * 128), 4096, wu, (fc // 2,))
        for dc in range(KC):
            for (a, b) in thirds:
                ws.push(wd.g.ap()[dc * 128:(dc + 1) * 128, a * 128:b * 128], (b - a) * 128, wd, (2 * dc, 2 * dc + 1))

    for ti in range(NT):
        t0 = ti * TT
        src = rl_all(x_dram, t0)
        S.op("pool", lambda E, src=src: E.dma_start(out=xb, in_=src), wr=[XB], dma=XB)
        for fc in range(NFC):
            b = fc % 2
            wgs, wgr = ws.next()
            pg, pgr = ps[b], psr[b]
            for kc in range(KC):
                S.op("pe", lambda E, o=pg, w=wgs[:, kc * 128:(kc + 1) * 128], x=xb[:, kc, :], kc=kc:
                     E.matmul(o[:], w, x, start=(kc == 0), stop=(kc == KC - 1)), rd=[wgr, XB], wr=[pgr])
            wus, wur = ws.next()
            pu, pur = ps[2 + b], psr[2 + b]
            for kc in range(KC):
                S.op("pe", lambda E, o=pu, w=wus[:, kc * 128:(kc + 1) * 128], x=xb[:, kc, :], kc=kc:
                     E.matmul(o[:], w, x, start=(kc == 0), stop=(kc == KC - 1)), rd=[wur, XB], wr=[pur])
            S.op("act", lambda E, o=sil[b], i=pg: E.activation(out=o, in_=i[:], func=AF.Silu),
                 rd=[pgr], wr=[R[16 + b]])
            S.op("dve", lambda E, o=HT[:, fc, :], a=sil[b], i=pu: E.tensor_tensor(out=o, in0=a, in1=i[:], op=ALU.mult),
                 rd=[R[16 + b], pur], wr=[H[fc]])
        for dc in range(KC):
            b = dc % 2
            po, por = ps[4 + b], psr[4 + b]
            xsrc = rl_one(x_dram, dc, t0)
            S.op("sp", lambda E, o=x32[:, b, :], i=xsrc: E.dma_start(out=o, in_=i), wr=[X32[b]], dma=X32[b])
            S.op("act", lambda E, o=x32[:, b, :]: E.activation(out=o, in_=o, func=AF.Copy, scale=float(ALPHA)),
                 rd=[X32[b]], wr=[X32[b]])
            for (a, bb) in thirds:
                wds, wdr = ws.next()
                for fc in range(a, bb):
                    S.op("pe", lambda E, o=po, w=wds[:, (fc - a) * 128:(fc - a + 1) * 128], h=HT[:, fc, :], fc=fc:
                         E.matmul(o[:], w, h, start=(fc == 0), stop=(fc == NFC - 1)), rd=[wdr, H[fc]], wr=[por])
            S.op("dve", lambda E, o=r[:, dc, :], i=po, x=x32[:, b, :]:
                 E.scalar_tensor_tensor(out=o, in0=i[:], scalar=0.5, in1=x, op0=ALU.mult, op1=ALU.add),
                 rd=[por, X32[b]], wr=[R[dc]])
        emit_ln(S, nc, A, r, R, lt, LT, ps, psr, LNP, CONST, y_dram, t0)


def emit_ln(S, nc, A, r, R, lt, LT, ps, psr, LNP, CONST, y_dram, t0, store_extra=None):
    ones = A["ones"]
    p1, p1r, p2, p2r = ps[6], psr[6], ps[7], psr[7]
    for dc in range(KC):
        b = dc % 2
        S.op("pe", lambda E, x=r[:, dc, :], dc=dc: E.matmul(p1[:], ones[:], x, start=(dc == 0), stop=(dc == KC - 1)),
             rd=[R[dc], CONST], wr=[p1r])
        S.op("act", lambda E, o=lt[b], x=r[:, dc, :]: E.activation(out=o, in_=x, func=AF.Square), rd=[R[dc]], wr=[LT[b]])
        S.op("pe", lambda E, x=lt[b], dc=dc: E.matmul(p2[:], ones[:], x, start=(dc == 0), stop=(dc == KC - 1)),
             rd=[LT[b], CONST], wr=[p2r])
    mean, msq, var, rstd = lt[2], lt[3], lt[4], lt[5]
    inv = 1.0 / (KC * 128)
    S.op("dve", lambda E: E.tensor_scalar(out=mean, in0=p1[:], scalar1=inv, scalar2=None, op0=ALU.mult), rd=[p1r], wr=[LT[2]])
    S.op("dve", lambda E: E.tensor_tensor(out=msq, in0=mean, in1=mean, op=ALU.mult), rd=[LT[2]], wr=[LT[3]])
    S.op("dve", lambda E: E.scalar_tensor_tensor(out=var, in0=p2[:], scalar=inv, in1=msq, op0=ALU.mult, op1=ALU.subtract),
         rd=[p2r, LT[3]], wr=[LT[4]])
    S.op("dve", lambda E: E.tensor_scalar_add(var, var, LN_EPS), rd=[LT[4]], wr=[LT[4]])
    S.op("act", lambda E: E.activation(out=var, in_=var, func=AF.Sqrt), rd=[LT[4]], wr=[LT[4]])
    S.op("dve", lambda E: E.reciprocal(out=rstd, in_=var), rd=[LT[4]], wr=[LT[5]])
    for dc in range(KC):
        b = dc % 2
        t, y = lt[6 + b], lt[8 + b]
        S.op("dve", lambda E, t=t, x=r[:, dc, :]: E.tensor_tensor(out=t, in0=x, in1=mean, op=ALU.subtract),
             rd=[R[dc], LT[2]], wr=[LT[6 + b]])
        S.op("dve", lambda E, t=t: E.tensor_tensor(out=t, in0=t, in1=rstd, op=ALU.mult), rd=[LT[6 + b], LT[5]], wr=[LT[6 + b]])
        S.op("act", lambda E, t=t, y=y, dc=dc: E.activation(out=y, in_=t, func=AF.Identity,
                                                          bias=A["lnb"][:, dc:dc + 1], scale=A["lnw"][:, dc:dc + 1]),
             rd=[LT[6 + b], LNP], wr=[LT[8 + b]])
        dst = rl_one(y_dram, dc, t0)
        S.op("sp", lambda E, y=y, dst=dst: E.dma_start(out=dst, in_=y), rd=[LT[8 + b]], wr=[S.out_res], dma=LT[8 + b])

HD = 128
NRH = 4
NAH = 4
SWA_PAT = ((128, 1), (512, 4), (2048, 16))
HCW = 512 + 128 + 2


def _mm(S, out, lhsT, rhs, rd, wr, start=True, stop=True):
    return S.op("pe", lambda E: E.matmul(out, lhsT, rhs, start=start, stop=stop), rd=rd, wr=wr)


def _act(S, out, in_, func, rd, wr, **kw):
    return S.op("act", lambda E: E.activation(out=out, in_=in_, func=func, **kw), rd=rd, wr=wr)


def _tt(S, out, a, b, op, rd, wr):
    return S.op("dve", lambda E: E.tensor_tensor(out=out, in0=a, in1=b, op=op), rd=rd, wr=wr)


def _stt(S, out, a, sc, b, op0, op1, rd, wr):
    return S.op("dve", lambda E: E.scalar_tensor_tensor(out=out, in0=a, scalar=sc, in1=b, op0=op0, op1=op1), rd=rd, wr=wr)


def _ts(S, out, a, s1, op0, rd, wr, s2=None, op1=None):
    if op1 is None:
        return S.op("dve", lambda E: E.tensor_scalar(out=out, in0=a, scalar1=s1, scalar2=None, op0=op0), rd=rd, wr=wr)
    return S.op("dve", lambda E: E.tensor_scalar(out=out, in0=a, scalar1=s1, scalar2=s2, op0=op0, op1=op1), rd=rd, wr=wr)


def x_tile_src(X1G, ti):
    q, hf, j = ti // 4, (ti % 4) // 2, ti % 2
    return X1G[:, hf, q, :, j * TT:(j + 1) * TT].rearrange("c p t -> p c t")


def emit_ret(S, nc, es, X1G, wmix, cos2, sin2, hconst, perm_d, ident_d, gnw_d, Y, NQ, dbg=9, nrh=NRH):
    SEQ = NQ * 2048
    NTI = SEQ // TT
    sb = lambda name, shape, dt: es.enter_context(nc.sbuf_tensor(name, shape, dt))
    pp = lambda name, shape, dt=F32: es.enter_context(nc.psum_tensor(name, shape, dt))
    R = S.res
    xb = sb("m_xb", [128, KC, TT], BF16); XB = R("m_xb")
    wbuf = sb("m_w", [128, 4, 4096], BF16); WB = [R(f"m_w{i}") for i in range(4)]
    ones = sb("m_ones", [128, 128], F32); onesb = sb("m_onesb", [128, 128], BF16)
    perm = sb("m_perm", [128, 128], F32)
    ident = sb("m_ident", [128, 128], BF16)
    gnw = sb("m_gnw", [128, NRH], F32)
    hc = sb("m_hc", [128, HCW], F32); HC = R("m_hc")
    CONST = R("m_const")
    S.op("dve", lambda E: E.memset(ones[:], 1.0), wr=[CONST])
    S.op("dve", lambda E: E.memset(onesb[:], 1.0), wr=[CONST])
    S.op("sp", lambda E: E.dma_start(out=perm[:], in_=perm_d), wr=[CONST], dma=CONST)
    CONSTI = R("m_consti")
    S.op("pool", lambda E: E.dma_start(out=ident[:], in_=ident_d), wr=[CONSTI], dma=CONSTI)
    S.op("sp", lambda E: E.dma_start(out=gnw[:], in_=gnw_d), wr=[CONST], dma=CONST)
    pA = [pp("m_pA0", [128, TT]), pp("m_pA1", [128, TT])]; PA = [R("m_pA0"), R("m_pA1")]
    pB = pp("m_pB", [128, TT]); PB = R("m_pB")
    pS1, PS1 = pB, PB
    pS2 = pp("m_pS2", [128, TT]); PS2 = R("m_pS2")
    pV = pp("m_pV", [128, TT])[:, 0:128]; PV = R("m_pV")
    pK = pp("m_pK", [128, TT])[:, 0:256]; PK = R("m_pK")
    pS = pp("m_pS", [128, TT])[:, 0:256]; PS = R("m_pS")
    pO = pp("m_pO", [128, TT])[:, 0:256]; PO = R("m_pO")
    state = {"pa": 0}

    def load_w(n, base):
        for i in range(n):
            S.op("pool", lambda E, i=i: E.dma_start(out=wbuf[:, i, :], in_=wmix[base + i]), wr=[WB[i]], dma=WB[i])

    def load_x(ti):
        src = x_tile_src(X1G, ti)
        S.op("pool", lambda E: E.dma_start(out=xb[:], in_=src), wr=[XB], dma=XB)

    def proj_fm(wi):
        b = state["pa"]; state["pa"] ^= 1
        for kc in range(KC):
            _mm(S, pA[b][:], wbuf[:, wi, kc * 128:(kc + 1) * 128], xb[:, kc, :], [WB[wi], XB], [PA[b]],
                start=(kc == 0), stop=(kc == KC - 1))
        return pA[b], PA[b]

    qf = sb("r_qf", [128, TT], F32); QF = R("r_qf")
    t1 = sb("r_t1", [128, TT], F32); T1 = R("r_t1")
    t2 = sb("r_t2", [128, TT], F32); T2 = R("r_t2")
    cs = sb("r_cs", [128, 2, TT], F32); CS = R("r_cs")
    qr = sb("r_qr", [128, TT], BF16); QR = R("r_qr")
    qx = sb("r_qx", [128, TT], BF16); QX = R("r_qx")
    kr = sb("r_kr", [128, TT], BF16); KR = R("r_kr")
    kz = sb("r_kz", [128, 4, 128], BF16); KZ = [R(f"r_kz{c}") for c in range(4)]
    vt = sb("r_vt", [128, 4, 128], BF16); VT = [R(f"r_vt{c}") for c in range(4)]
    gs = sb("r_gs", [128, TT], F32); GS = R("r_gs")
    ptb = sb("r_pt", [128, 128], BF16); PTB = R("r_pt")
    ot = sb("r_o", [128, TT], F32); OT = R("r_o")
    sq = sb("r_sq", [128, TT], F32); SQ = R("r_sq")
    mean = sb("r_mean", [128, TT], F32); MEAN = R("r_mean")
    var = sb("r_var", [128, TT], F32); VAR = R("r_var")
    yb = sb("r_yb", [128, TT], F32); YB = R("r_yb")
    st = sb("r_st", [128, 128], F32); ST = R("r_st")
    stb = sb("r_stb", [128, 128], BF16); STB = R("r_stb")
    YD = R("Ydram")

    def rope(pin, PIN, dst, DST, extra=None, EXTRA=None, scale=1.0):
        _act(S, qf[:], pin[:], AF.Copy, [PIN], [QF], scale=float(scale))
        _mm(S, pB[:], perm[:], qf[:], [CONST, QF], [PB])
        _tt(S, t1[:], qf[:], cs[:, 0, :], ALU.mult, [QF, CS], [T1])
        _tt(S, t2[:], pB[:], cs[:, 1, :], ALU.mult, [PB, CS], [T2])
        _tt(S, dst[:], t1[:], t2[:], ALU.add, [T1, T2], [DST])
        if extra is not None:
            _tt(S, t1[:], t1[:], t2[:], ALU.add, [T1, T2], [T1])
            _tt(S, extra[:], t1[:], hc[:, 0:512], ALU.mult, [T1, HC], [EXTRA])

    for a in range(nrh):
        load_w(4, 4 * a)
        S.op("sp", lambda E, a=a: E.dma_start(out=hc[:], in_=hconst[a]), wr=[HC], dma=HC)
        S.op("dve", lambda E: E.memset(st[:], 0.0), wr=[ST])
        S.op("dve", lambda E: E.memset(stb[:], 0.0), wr=[STB])
        for ti in range(NTI):
            T0 = ti * TT
            load_x(ti)
            S.op("sp", lambda E, T0=T0: E.dma_start(out=cs[:, 0, :], in_=cos2[:, T0:T0 + TT]), wr=[CS], dma=CS)
            S.op("sp", lambda E, T0=T0: E.dma_start(out=cs[:, 1, :], in_=sin2[:, T0:T0 + TT]), wr=[CS], dma=CS)
            if dbg < 1:
                continue
            p, P = proj_fm(0)
            rope(p, P, qr, QR, qx, QX)
            p, P = proj_fm(1)
            rope(p, P, kr, KR, scale=HD ** -0.5)
            p, P = proj_fm(3)
            _act(S, gs[:], p[:], AF.Silu, [P], [GS])
            for c in range(4 if dbg >= 2 else 0):
                cs_ = slice(c * 128, (c + 1) * 128)
                for kc in range(KC):
                    _mm(S, pV, xb[:, kc, cs_], wbuf[:, 2, kc * 128:(kc + 1) * 128], [XB, WB[2]], [PV],
                        start=(kc == 0), stop=(kc == KC - 1))
                _act(S, vt[:, c, :], pV, AF.Copy, [PV], [VT[c]])
                _mm(S, pK[:, 0:128], kr[:, cs_], ident[:], [KR, CONSTI], [PK])
                _ts(S, kz[:, c, :], pK[:, 0:128], hc[:, 640:641], ALU.mult, [PK, HC], [KZ[c]])
                _mm(S, pS[:, 0:128], kr[:, cs_], qr[:, cs_], [KR, QR], [PS])
                _tt(S, ptb[:], pS[:, 0:128], hc[:, 512:640], ALU.mult, [PS, HC], [PTB])
                _mm(S, pO[:, 0:128], vt[:, c, :], ptb[:], [VT[c], PTB], [PO], start=True, stop=False)
                _mm(S, pO[:, 0:128], stb[:], qx[:, cs_], [STB, QX], [PO], start=False, stop=True)
                _act(S, ot[:, cs_], pO[:, 0:128], AF.Copy, [PO], [OT])
                _mm(S, pK[:, 128:256], kz[:, c, :], vt[:, c, :], [KZ[c], VT[c]], [PK])
                _stt(S, st[:], st[:], hc[:, 641:642], pK[:, 128:256], ALU.mult, ALU.add, [ST, HC, PK], [ST])
                _act(S, stb[:], st[:], AF.Copy, [ST], [STB])
            if dbg < 3:
                continue
            _mm(S, pS1[:], ones[:], ot[:], [CONST, OT], [PS1])
            _act(S, sq[:], ot[:], AF.Square, [OT], [SQ])
            _mm(S, pS2[:], ones[:], sq[:], [CONST, SQ], [PS2])
            _ts(S, mean[:], pS1[:], 1.0 / HD, ALU.mult, [PS1], [MEAN])
            _tt(S, sq[:], mean[:], mean[:], ALU.mult, [MEAN], [SQ])
            _stt(S, var[:], pS2[:], 1.0 / HD, sq[:], ALU.mult, ALU.subtract, [PS2, SQ], [VAR])
            S.op("dve", lambda E: E.tensor_scalar_add(var[:], var[:], 1e-5), rd=[VAR], wr=[VAR])
            _act(S, var[:], var[:], AF.Sqrt, [VAR], [VAR])
            S.op("dve", lambda E: E.reciprocal(out=var[:], in_=var[:]), rd=[VAR], wr=[VAR])
            _tt(S, ot[:], ot[:], mean[:], ALU.subtract, [OT, MEAN], [OT])
            _tt(S, ot[:], ot[:], var[:], ALU.mult, [OT, VAR], [OT])
            _stt(S, yb[:], ot[:], gnw[:, a:a + 1], gs[:], ALU.mult, ALU.mult, [OT, CONST, GS], [YB])
            S.op("sp", lambda E, a=a, T0=T0: E.dma_start(out=Y[a, :, T0:T0 + TT], in_=yb[:]), rd=[YB], wr=[YD], dma=YB)
    return YD


def emit_swa(S, nc, es, X1G, wmix, wbase, band_d, ident_d, Y, NQ, nah=NAH):
    SEQ = NQ * 2048
    NTI = SEQ // TT
    sb = lambda name, shape, dt: es.enter_context(nc.sbuf_tensor(name, shape, dt))
    pp = lambda name, shape, dt=F32: es.enter_context(nc.psum_tensor(name, shape, dt))
    R = S.res
    xb = sb("a_xb", [128, KC, TT], BF16); XB = R("a_xb")
    wbuf = sb("a_w", [128, 3, 4096], BF16); WB = [R(f"a_w{i}") for i in range(3)]
    onesb = sb("a_onesb", [128, 128], BF16); band = sb("a_band", [128, 256], F32); ident = sb("a_ident", [128, 128], BF16)
    CONST = R("a_const")
    S.op("dve", lambda E: E.memset(onesb[:], 1.0), wr=[CONST])
    S.op("sp", lambda E: E.dma_start(out=band[:], in_=band_d), wr=[CONST], dma=CONST)
    CONSTI = R("a_consti")
    S.op("pool", lambda E: E.dma_start(out=ident[:], in_=ident_d), wr=[CONSTI], dma=CONSTI)
    QT = sb("a_QT", [128, SEQ], BF16); KT = sb("a_KT", [128, SEQ], BF16); VTt = sb("a_VT", [128, SEQ], BF16)
    NUM = sb("a_NUM", [128, SEQ], F32); DEN = sb("a_DEN", [128, SEQ], F32)
    RQ, RK, RV, RN, RD = R("a_QT"), R("a_KT"), R("a_VT"), R("a_NUM"), R("a_DEN")
    vb = sb("a_vb", [128, 128], BF16); VB = R("a_vb")
    ef = sb("a_ef", [128, 256], F32); EF = R("a_ef")
    em = sb("a_em", [128, 256], BF16); EM = R("a_em")
    rec = sb("a_rec", [128, TT], F32); REC = R("a_rec")
    yb = sb("a_yb", [128, TT], F32); YB = R("a_yb")
    pA = [pp("a_pA0", [128, TT]), pp("a_pA1", [128, TT])]; PA = [R("a_pA0"), R("a_pA1")]
    pV = pp("a_pV", [128, TT])[:, 0:128]; PV = R("a_pV")
    pS = pp("a_pS", [128, TT])[:, 0:256]; PS = R("a_pS")
    pN = pp("a_pN", [128, TT])[:, 0:256]; PN = R("a_pN")
    pD = pp("a_pD", [128, TT])[:, 0:256]; PD = R("a_pD")
    YD = R("Ydram_a")
    pa = [0]
    for a in range(nah):
        for i in range(3):
            S.op("pool", lambda E, i=i, a=a: E.dma_start(out=wbuf[:, i, :], in_=wmix[wbase + 3 * a + i]), wr=[WB[i]], dma=WB[i])
        for ti in range(NTI):
            T0 = ti * TT
            src = x_tile_src(X1G, ti)
            S.op("pool", lambda E, src=src: E.dma_start(out=xb[:], in_=src), wr=[XB], dma=XB)
            for wi, (dst, DR, sc) in enumerate(((QT, RQ, HD ** -0.5), (KT, RK, 1.0), (VTt, RV, 1.0))):
                b = pa[0]; pa[0] ^= 1
                for kc in range(KC):
                    _mm(S, pA[b][:], wbuf[:, wi, kc * 128:(kc + 1) * 128], xb[:, kc, :], [WB[wi], XB], [PA[b]],
                        start=(kc == 0), stop=(kc == KC - 1))
                _act(S, dst[:, T0:T0 + TT], pA[b][:], AF.Copy, [PA[b]], [DR], scale=float(sc))
        S.op("dve", lambda E: E.memset(NUM[:], 0.0), wr=[RN])
        S.op("dve", lambda E: E.memset(DEN[:], 0.0), wr=[RD])
        for (window, r) in SWA_PAT:
            nb = SEQ // r // 128
            for rho in range(r):
                for m in range(nb):
                    k0 = rho + r * 128 * m
                    keys = slice(k0, k0 + r * 127 + 1, r)
                    nq = 256 if m + 1 < nb else 128
                    qs = slice(k0, k0 + r * (nq - 1) + 1, r)
                    _mm(S, pV, VTt[:, keys], ident[:], [RV, CONSTI], [PV])
                    _act(S, vb[:], pV, AF.Copy, [PV], [VB])
                    _mm(S, pS[:, 0:nq], KT[:, keys], QT[:, qs], [RK, RQ], [PS])
                    _act(S, ef[:, 0:nq], pS[:, 0:nq], AF.Exp, [PS], [EF])
                    _tt(S, em[:, 0:nq], ef[:, 0:nq], band[:, 0:nq], ALU.mult, [EF, CONST], [EM])
                    _mm(S, pN[:, 0:nq], vb[:], em[:, 0:nq], [VB, EM], [PN])
                    _mm(S, pD[:, 0:nq], onesb[:], em[:, 0:nq], [CONST, EM], [PD])
                    _tt(S, NUM[:, qs], NUM[:, qs], pN[:, 0:nq], ALU.add, [RN, PN], [RN])
                    _tt(S, DEN[:, qs], DEN[:, qs], pD[:, 0:nq], ALU.add, [RD, PD], [RD])
        for ti in range(NTI):
            T0 = ti * TT
            S.op("dve", lambda E, T0=T0: E.reciprocal(out=rec[:], in_=DEN[:, T0:T0 + TT]), rd=[RD], wr=[REC])
            _tt(S, yb[:], NUM[:, T0:T0 + TT], rec[:], ALU.mult, [RN, REC], [YB])
            S.op("sp", lambda E, a=a, T0=T0: E.dma_start(out=Y[NRH + a, :, T0:T0 + TT], in_=yb[:]), rd=[YB], wr=[YD], dma=YB)
    return YD

HD = 128
NDH = 8
DCW = 12 + 2 + 1


def emit_gdn(S, nc, es, X1G, wmix, wab_d, hconst, mats_d, ident_d, Y, NQ, nh=NDH, nti=None):
    SEQ = NQ * 2048
    NTI = nti or SEQ // TT
    sb = lambda name, shape, dt: es.enter_context(nc.sbuf_tensor(name, shape, dt))
    pp = lambda name: es.enter_context(nc.psum_tensor(name, [128, TT], F32))
    R = S.res
    xb = sb("d_xb", [128, KC, TT], BF16); XB = R("d_xb")
    wbuf = sb("d_w", [128, 4, 4096], BF16); WB = [R(f"d_w{i}") for i in range(4)]
    wab = sb("d_wab", [128, KC, 2 * NDH], BF16); WAB = R("d_wab")
    mats = sb("d_mats", [128, 6, 128], F32); MATS = R("d_mats")
    identb = sb("d_identb", [128, 128], BF16); IDB = R("d_identb")
    ones = sb("d_ones", [128, 128], F32); negones = sb("d_negones", [128, 128], F32); CONST = R("d_const")
    sel = sb("d_sel", [128, 2], F32)
    hc = sb("d_hc", [128, DCW], F32); HC = R("d_hc")
    S.op("dve", lambda E: E.memset(ones[:], 1.0), wr=[CONST])
    S.op("dve", lambda E: E.memset(negones[:], -1.0), wr=[CONST])
    S.op("dve", lambda E: E.memset(sel[:], 0.0), wr=[CONST])
    S.op("dve", lambda E: E.memset(sel[0:64, 0:1], 1.0), wr=[CONST])
    S.op("dve", lambda E: E.memset(sel[64:128, 1:2], 1.0), wr=[CONST])
    S.op("sp", lambda E: E.dma_start(out=mats[:], in_=mats_d.rearrange("k p c -> p k c")), wr=[MATS], dma=MATS)
    S.op("pool", lambda E: E.dma_start(out=identb[:], in_=ident_d), wr=[IDB], dma=IDB)
    S.op("pool", lambda E: E.dma_start(out=wab[:], in_=wab_d), wr=[WAB], dma=WAB)
    U, BONES, NEGLOW, NEGUP, STRICT, IDENT = (mats[:, k, :] for k in range(6))
    pA = [pp("d_pA0"), pp("d_pA1")]; PA = [R("d_pA0"), R("d_pA1")]
    pB = pp("d_pB"); PB = R("d_pB")
    pC = pp("d_pC"); PC = R("d_pC")
    pD = pp("d_pD"); PD = R("d_pD")
    pE = pp("d_pE"); PE_ = R("d_pE")
    pF = pp("d_pF"); PF = R("d_pF")
    pG = pp("d_pG"); PG = R("d_pG")
    pa = [0]
    raw = [sb(f"d_raw{i}", [128, 3 + TT], F32) for i in range(3)]; RAW = [R(f"d_raw{i}") for i in range(3)]
    acc = sb("d_acc", [128, TT], F32); ACC = R("d_acc")
    sq = sb("d_sq", [128, TT], F32); SQ = R("d_sq")
    rn = sb("d_rn", [128, TT], F32); RN = R("d_rn")
    qn = sb("d_qn", [128, TT], F32); QN = R("d_qn")
    qnb = sb("d_qnb", [128, TT], BF16); QNB = R("d_qnb")
    knb = sb("d_knb", [128, TT], BF16); KNB = R("d_knb")
    vcb = sb("d_vcb", [128, TT], BF16); VCB = R("d_vcb")
    gs = sb("d_gs", [128, TT], F32); GS = R("d_gs")
    ot = sb("d_ot", [128, TT], F32); OT = R("d_ot")
    yb = sb("d_yb", [128, TT], F32); YB = R("d_yb")
    ab = sb("d_ab", [128, 2], F32); AB = R("d_ab")
    beta = sb("d_beta", [128, 1], F32); BETA = R("d_beta")
    nbeta = sb("d_nbeta", [128, 1], F32); NBETA = R("d_nbeta")
    gcol = sb("d_g", [128, 1], F32); GC = R("d_g")
    nega = sb("d_nega", [128, 1], F32); NEGA = R("d_nega")
    ug = sb("d_ug", [128, 128], F32); UG = R("d_ug")
    gsel = sb("d_gsel", [128, 2], F32); GSEL = R("d_gsel")
    cols = sb("d_cols", [128, 4], F32); COLS = R("d_cols")
    egl = sb("d_egl", [128, 2], F32); EGL = R("d_egl")
    dm = sb("d_dm", [128, 128], F32); DM = R("d_dm")
    el = sb("d_el", [128, 128], F32); EL = R("d_el")
    et = sb("d_et", [128, 128], F32); ET = R("d_et")
    eg = sb("d_eg", [128, 128], F32); EG = R("d_eg")
    ktm = sb("d_ktm", [128, 128], F32); KTM = R("d_ktm")
    kbg = sb("d_kbg", [128, 128], BF16); KBG = R("d_kbg")
    kd = sb("d_kd", [128, 128], BF16); KD = R("d_kd")
    vbt = sb("d_vbt", [128, 128], BF16); VBT = R("d_vbt")
    X = [sb(f"d_X{i}", [128, 128], F32) for i in range(2)]; XR = [R(f"d_X{i}") for i in range(2)]
    XT = [sb(f"d_XT{i}", [128, 128], F32) for i in range(2)]; XTR = [R(f"d_XT{i}") for i in range(2)]
    Yk = sb("d_Yk", [128, 128], F32); YK = R("d_Yk")
    Q = [sb(f"d_Q{i}", [128, 128], F32) for i in range(2)]; QRs = [R(f"d_Q{i}") for i in range(2)]
    ttb = sb("d_ttb", [128, 128], BF16); TTB = R("d_ttb")
    ub = sb("d_u", [128, 128], F32); UB = R("d_u")
    wtb = sb("d_wt", [128, 128], BF16); WTB = R("d_wt")
    qgb = sb("d_qg", [128, 128], BF16); QGB = R("d_qg")
    qkt = sb("d_qkt", [128, 128], BF16); QKT = R("d_qkt")
    vnew = sb("d_vnew", [128, 128], BF16); VNEW = R("d_vnew")
    st = sb("d_st", [128, 128], F32); ST = R("d_st")
    stb = sb("d_stb", [128, 128], BF16); STB = R("d_stb")
    YD = R("Ydram_d")

    def proj_fm(wi):
        b = pa[0]; pa[0] ^= 1
        for kc in range(KC):
            _mm(S, pA[b][:], wbuf[:, wi, kc * 128:(kc + 1) * 128], xb[:, kc, :], [WB[wi], XB], [PA[b]],
                start=(kc == 0), stop=(kc == KC - 1))
        return pA[b], PA[b]

    def conv_silu(i, pin, PIN, cw0):
        r_, R_ = raw[i], RAW[i]
        _act(S, r_[:, 3:3 + TT], pin[:], AF.Copy, [PIN], [R_])
        _ts(S, acc[:], r_[:, 0:TT], hc[:, cw0:cw0 + 1], ALU.mult, [R_, HC], [ACC])
        for k in range(1, 4):
            _stt(S, acc[:], r_[:, k:k + TT], hc[:, cw0 + k:cw0 + k + 1], acc[:], ALU.mult, ALU.add, [R_, HC, ACC], [ACC])
        _act(S, r_[:, 0:3], r_[:, TT:TT + 3], AF.Copy, [R_], [R_])
        _act(S, acc[:], acc[:], AF.Silu, [ACC], [ACC])

    def l2n(dst, DST, scale):
        _act(S, sq[:], acc[:], AF.Square, [ACC], [SQ])
        _mm(S, pB[:], ones[:], sq[:], [CONST, SQ], [PB])
        _ts(S, rn[:], pB[:], 1e-6, ALU.add, [PB], [RN])
        _act(S, rn[:], rn[:], AF.Sqrt, [RN], [RN])
        S.op("dve", lambda E: E.reciprocal(out=rn[:], in_=rn[:]), rd=[RN], wr=[RN])
        _stt(S, dst[:], acc[:], float(scale), rn[:], ALU.mult, ALU.mult, [ACC, RN], [DST])

    for a in range(nh):
        for i in range(4):
            S.op("pool", lambda E, i=i, a=a: E.dma_start(out=wbuf[:, i, :], in_=wmix[4 * a + i]), wr=[WB[i]], dma=WB[i])
        S.op("sp", lambda E, a=a: E.dma_start(out=hc[:], in_=hconst[a]), wr=[HC], dma=HC)
        S.op("dve", lambda E: E.memset(st[:], 0.0), wr=[ST])
        S.op("dve", lambda E: E.memset(stb[:], 0.0), wr=[STB])
        for i in range(3):
            S.op("dve", lambda E, i=i: E.memset(raw[i][:, 0:3], 0.0), wr=[RAW[i]])
        _act(S, nega[:], hc[:, 12:13], AF.Exp, [HC], [NEGA])
        _ts(S, nega[:], nega[:], -1.0, ALU.mult, [NEGA], [NEGA])
        for ti in range(NTI):
            T0 = ti * TT
            src = x_tile_src(X1G, ti)
            S.op("pool", lambda E, src=src: E.dma_start(out=xb[:], in_=src), wr=[XB], dma=XB)
            p, P = proj_fm(0); conv_silu(0, p, P, 0); l2n(qn, QN, HD ** -0.5)
            _act(S, qnb[:], qn[:], AF.Copy, [QN], [QNB])
            p, P = proj_fm(1); conv_silu(1, p, P, 4); l2n(knb, KNB, 1.0)
            p, P = proj_fm(2); conv_silu(2, p, P, 8)
            _act(S, vcb[:], acc[:], AF.Copy, [ACC], [VCB])
            p, P = proj_fm(3)
            _act(S, gs[:], p[:], AF.Silu, [P], [GS])
            for c in range(4):
                cs_ = slice(c * 128, (c + 1) * 128)
                for kc in range(KC):
                    _mm(S, pC[:, 0:1], xb[:, kc, cs_], wab[:, kc, a:a + 1], [XB, WAB], [PC], start=(kc == 0), stop=(kc == KC - 1))
                for kc in range(KC):
                    _mm(S, pC[:, 1:2], xb[:, kc, cs_], wab[:, kc, NDH + a:NDH + a + 1], [XB, WAB], [PC], start=(kc == 0), stop=(kc == KC - 1))
                _act(S, ab[:], pC[:, 0:2], AF.Copy, [PC], [AB])
                _act(S, beta[:], ab[:, 1:2], AF.Sigmoid, [AB], [BETA])
                _ts(S, nbeta[:], beta[:], -1.0, ALU.mult, [BETA], [NBETA])
                _act(S, gcol[:], ab[:, 0:1], AF.Exp, [AB, HC], [GC], bias=hc[:, 13:14])
                _ts(S, gcol[:], gcol[:], 1.0, ALU.add, [GC], [GC])
                _act(S, gcol[:], gcol[:], AF.Ln, [GC], [GC])
                _tt(S, gcol[:], gcol[:], nega[:], ALU.mult, [GC, NEGA], [GC])
                _ts(S, ug[:], U, gcol[:, 0:1], ALU.mult, [MATS, GC], [UG])
                _ts(S, gsel[:], sel[:], gcol[:, 0:1], ALU.mult, [CONST, GC], [GSEL])
                _mm(S, pC[:, 8:9], U, gcol[:], [MATS, GC], [PC])
                _mm(S, pC[:, 9:10], BONES, gcol[:], [MATS, GC], [PC])
                _mm(S, pC[:, 16:18], ones[:], gsel[:], [CONST, GSEL], [PC])
                _act(S, cols[:, 0:2], pC[:, 8:10], AF.Copy, [PC], [COLS])
                _act(S, egl[:], pC[:, 16:18], AF.Exp, [PC], [EGL])
                _act(S, cols[:, 2:3], cols[:, 0:1], AF.Exp, [COLS], [COLS])
                _tt(S, cols[:, 3:4], cols[:, 1:2], cols[:, 0:1], ALU.subtract, [COLS], [COLS])
                _act(S, cols[:, 3:4], cols[:, 3:4], AF.Exp, [COLS], [COLS])
                _mm(S, pD[:, 0:128], ug[:], ones[:], [UG, CONST], [PD], start=True, stop=False)
                _mm(S, pD[:, 0:128], negones[:], ug[:], [UG, CONST], [PD], start=False, stop=True)
                _tt(S, dm[:], pD[:, 0:128], NEGLOW, ALU.add, [PD, MATS], [DM])
                _act(S, el[:], dm[:], AF.Exp, [DM], [EL])
                _tt(S, el[:], el[:], STRICT, ALU.mult, [EL, MATS], [EL])
                _mm(S, pD[:, 128:256], ones[:], ug[:], [UG, CONST], [PD], start=True, stop=False)
                _mm(S, pD[:, 128:256], ug[:], negones[:], [UG, CONST], [PD], start=False, stop=True)
                _tt(S, dm[:], pD[:, 128:256], NEGUP, ALU.add, [PD, MATS], [DM])
                _act(S, et[:], dm[:], AF.Exp, [DM], [ET])
                _mm(S, pD[:, 256:384], ones[:], ug[:], [UG, CONST], [PD])
                _act(S, eg[:], pD[:, 256:384], AF.Exp, [PD], [EG])
                _mm(S, pC[:, 128:256], knb[:, cs_], identb[:], [KNB, IDB], [PC])
                _act(S, ktm[:], pC[:, 128:256], AF.Copy, [PC], [KTM])
                _ts(S, kbg[:], ktm[:], beta[:, 0:1], ALU.mult, [KTM, BETA, COLS], [KBG], s2=cols[:, 2:3], op1=ALU.mult)
                _ts(S, kd[:], ktm[:], cols[:, 3:4], ALU.mult, [KTM, COLS], [KD])
                _mm(S, pC[:, 256:384], vcb[:, cs_], identb[:], [VCB, IDB], [PC])
                _ts(S, vbt[:], pC[:, 256:384], beta[:, 0:1], ALU.mult, [PC, BETA], [VBT])
                _mm(S, pE[:, 0:128], knb[:, cs_], knb[:, cs_], [KNB], [PE_])
                _stt(S, X[0][:], pE[:, 0:128], nbeta[:, 0:1], el[:], ALU.mult, ALU.mult, [PE_, NBETA, EL], [XR[0]])
                _mm(S, pE[:, 128:256], X[0][:], IDENT, [XR[0], MATS], [PE_])
                _act(S, XT[0][:], pE[:, 128:256], AF.Copy, [PE_], [XTR[0]])
                _tt(S, Q[0][:], XT[0][:], IDENT, ALU.add, [XTR[0], MATS], [QRs[0]])
                cur = 0
                for lvl in range(5):
                    nxt = cur ^ 1
                    _mm(S, pE[:, 0:128], XT[cur][:], X[cur][:], [XTR[cur], XR[cur]], [PE_])
                    _mm(S, pE[:, 128:256], X[cur][:], XT[cur][:], [XTR[cur], XR[cur]], [PE_])
                    _act(S, X[nxt][:], pE[:, 0:128], AF.Copy, [PE_], [XR[nxt]])
                    _act(S, XT[nxt][:], pE[:, 128:256], AF.Copy, [PE_], [XTR[nxt]])
                    _tt(S, Yk[:], X[nxt][:], IDENT, ALU.add, [XR[nxt], MATS], [YK])
                    _mm(S, pE[:, 256:384], Yk[:], Q[cur][:], [YK, QRs[cur]], [PE_])
                    _act(S, Q[nxt][:], pE[:, 256:384], AF.Copy, [PE_], [QRs[nxt]])
                    cur = nxt
                _act(S, ttb[:], Q[cur][:], AF.Copy, [QRs[cur]], [TTB])
                _mm(S, pF[:, 0:128], ttb[:], vbt[:], [TTB, VBT], [PF])
                _act(S, ub[:], pF[:, 0:128], AF.Copy, [PF], [UB])
                _mm(S, pF[:, 128:256], kbg[:], ttb[:], [KBG, TTB], [PF])
                _act(S, wtb[:], pF[:, 128:256], AF.Copy, [PF], [WTB])
                _tt(S, qgb[:], qn[:, cs_], eg[:], ALU.mult, [QN, EG], [QGB])
                _mm(S, pF[:, 256:384], knb[:, cs_], qnb[:, cs_], [KNB, QNB], [PF])
                _tt(S, qkt[:], pF[:, 256:384], et[:], ALU.mult, [PF, ET], [QKT])
                for h2 in range(2):
                    ps_ = slice(h2 * 64, (h2 + 1) * 64)
                    _mm(S, pF[:, 0:128], wtb[:], stb[:], [WTB, STB], [PF])
                    _tt(S, vnew[ps_, :], ub[ps_, :], pF[ps_, 0:128], ALU.subtract, [UB, PF], [VNEW])
                    _mm(S, pG[:, c * 128 + h2 * 64:c * 128 + (h2 + 1) * 64], stb[:], qgb[:, ps_], [STB, QGB], [PG], start=True, stop=False)
                    _mm(S, pG[:, c * 128 + h2 * 64:c * 128 + (h2 + 1) * 64], vnew[ps_, :], qkt[ps_, ps_], [VNEW, QKT], [PG], start=False, stop=True)
                    _mm(S, pF[:, 128:256], kd[ps_, :], vnew[ps_, :], [KD, VNEW], [PF])
                    _stt(S, st[:], st[:], egl[:, h2:h2 + 1], pF[:, 128:256], ALU.mult, ALU.add, [ST, EGL, PF], [ST])
                    _act(S, stb[:], st[:], AF.Copy, [ST], [STB])
            _act(S, ot[:], pG[:], AF.Copy, [PG], [OT])
            _act(S, sq[:], ot[:], AF.Square, [OT], [SQ])
            _mm(S, pB[:], ones[:], sq[:], [CONST, SQ], [PB])
            _ts(S, rn[:], pB[:], 1.0 / HD, ALU.mult, [PB], [RN], s2=1e-6, op1=ALU.add)
            _act(S, rn[:], rn[:], AF.Sqrt, [RN], [RN])
            S.op("dve", lambda E: E.reciprocal(out=rn[:], in_=rn[:]), rd=[RN], wr=[RN])
            _tt(S, ot[:], ot[:], rn[:], ALU.mult, [OT, RN], [OT])
            _stt(S, yb[:], ot[:], hc[:, 14:15], gs[:], ALU.mult, ALU.mult, [OT, HC, GS], [YB])
            S.op("sp", lambda E, a=a, T0=T0: E.dma_start(out=Y[a, :, T0:T0 + TT], in_=yb[:]), rd=[YB], wr=[YD], dma=YB)
    return YD


def emit_allgather_x(S, nc, X, XG, ccres):
    XR = S.res("agx_src"); XGR = S.res("agx_dst")
    for i in range(KC * 2):
        src = X.ap()[i * 128:(i + 1) * 128, :]
        dst = XG.ap()[i * 512:(i + 1) * 512, :]
        S.op("pool", lambda E, src=src, dst=dst: E.collective_compute(
            "AllGather", ALU.bypass, replica_groups=[[0, 1, 2, 3], [4, 5, 6, 7]], ins=[src.opt()], outs=[dst.opt()]),
            rd=[XR], wr=[XGR], dma=ccres, cc=True)


def emit_reducescatter(S, nc, P, M, ccres):
    PR = S.res("rs_src"); MR = S.res("rs_dst")
    for dc in range(KC):
        src = P.ap()[dc * 512:(dc + 1) * 512, :]
        dst = M.ap()[dc * 128:(dc + 1) * 128, :]
        S.op("pool", lambda E, src=src, dst=dst: E.collective_compute(
            "ReduceScatter", ALU.add, replica_groups=[[0, 1, 2, 3], [4, 5, 6, 7]], ins=[src.opt()], outs=[dst.opt()]),
            rd=[PR], wr=[MR], dma=ccres, cc=True)


def emit_outproj(S, nc, es, Y, wo_d, P, NQ=4, sfx=""):
    SEQ = NQ * 2048
    sb = lambda name, shape, dt: es.enter_context(nc.sbuf_tensor(name + sfx, shape, dt))
    R = S.res
    wo = sb("o_wo", [128, 8, 4096], BF16); WO = R("o_wo")
    yb = [sb(f"o_yb{i}", [128, 8, TT], BF16) for i in range(2)]; YB = [R(f"o_yb{i}") for i in range(2)]
    ob = [sb(f"o_ob{i}", [128, TT], F32) for i in range(2)]; OB = [R(f"o_ob{i}") for i in range(2)]
    ps = [es.enter_context(nc.psum_tensor(f"o_ps{i}" + sfx, [128, TT], F32)) for i in range(2)]; PS = [R(f"o_ps{i}") for i in range(2)]
    PD = R("o_P")
    S.op("pool", lambda E: E.dma_start(out=wo[:], in_=wo_d.rearrange("k p c -> p k c")), wr=[WO], dma=WO)
    n = 0
    for ti in range(SEQ // TT):
        T0 = ti * TT
        q, o = T0 // 2048, T0 % 2048
        b = ti % 2
        S.op("pool", lambda E, b=b, T0=T0: E.dma_start(out=yb[b][:], in_=Y[:, :, T0:T0 + TT].rearrange("k p t -> p k t")),
             wr=[YB[b]], dma=YB[b])
        for dc in range(KC):
            k = n % 2; n += 1
            for kc in range(8):
                _mm(S, ps[k][:], wo[:, kc, dc * 128:(dc + 1) * 128], yb[b][:, kc, :], [WO, YB[b]], [PS[k]],
                    start=(kc == 0), stop=(kc == 7))
            _act(S, ob[k][:], ps[k][:], AF.Copy, [PS[k]], [OB[k]])
            S.op("sp", lambda E, k=k, dc=dc, q=q, o=o: E.dma_start(out=P[dc, q, :, o:o + TT], in_=ob[k][:]),
                 rd=[OB[k]], wr=[PD], dma=OB[k])


def emit_resid_ln(S, nc, es, Xin, M, Xout, lnw, lnb, NT, sfx=""):
    sb = lambda name, shape, dt: es.enter_context(nc.sbuf_tensor(name + sfx, shape, dt))
    R = S.res
    A = {}
    A["ones"] = sb("l_ones", [128, 128], F32)
    A["lnw"] = sb("l_lnw", [128, KC], F32); A["lnb"] = sb("l_lnb", [128, KC], F32)
    rbuf = sb("l_r", [128, KC * TT], F32)
    r = rbuf[:].rearrange("p (c t) -> p c t", c=KC)
    Rr = [R(f"l_R{d}") for d in range(KC)]
    ltb = sb("l_lt", [128, 10, TT], F32); lt = [ltb[:, k, :] for k in range(10)]; LT = [R(f"l_LT{k}") for k in range(10)]
    x32 = sb("l_x32", [128, 2, TT], F32); X32 = [R("l_x32a"), R("l_x32b")]
    m32 = sb("l_m32", [128, 2, TT], F32); M32 = [R("l_m32a"), R("l_m32b")]
    ps = [None] * 6 + [es.enter_context(nc.psum_tensor("l_ps6" + sfx, [128, TT], F32)), es.enter_context(nc.psum_tensor("l_ps7" + sfx, [128, TT], F32))]
    psr = [None] * 6 + [R("l_ps6"), R("l_ps7")]
    CONST = R("l_const"); LNP = R("l_lnp")
    S.op("dve", lambda E: E.memset(A["ones"][:], 1.0), wr=[CONST])
    S.op("sp", lambda E: E.dma_start(out=A["lnw"][:], in_=lnw), wr=[LNP], dma=LNP)
    S.op("sp", lambda E: E.dma_start(out=A["lnb"][:], in_=lnb), wr=[LNP], dma=LNP)
    for ti in range(NT):
        t0 = ti * TT
        for dc in range(KC):
            b = dc % 2
            xs_ = rl_one(Xin, dc, t0)
            ms_ = M[dc, :, t0:t0 + TT]
            S.op("sp", lambda E, b=b, xs_=xs_: E.dma_start(out=x32[:, b, :], in_=xs_), wr=[X32[b]], dma=X32[b])
            S.op("sp", lambda E, b=b, ms_=ms_: E.dma_start(out=m32[:, b, :], in_=ms_), wr=[M32[b]], dma=M32[b])
            _stt(S, r[:, dc, :], x32[:, b, :], float(ALPHA), m32[:, b, :], ALU.mult, ALU.add, [X32[b], M32[b]], [Rr[dc]])
        emit_ln(S, nc, A, r, Rr, lt, LT, ps, psr, LNP, CONST, Xout, t0)

D_MODEL = 4096
D_FF = 11008
NFC = D_FF // 128
NCORE = 8
TOK_PER_CORE = 2048
NT = TOK_PER_CORE // TT
SEQ = 8192
NSCW = NFC // 2

_PROG = None


def _build(debug=False):
    nc = bass.Bass("TRN2", target_bir_lowering=False)

    def ext(name, shape):
        return nc.dram_tensor(name, shape, F32, kind="ExternalInput").ap()

    RL = [KC * 2 * 128, 1024]
    rlv = lambda t: t.ap().rearrange("(c h p) t -> c h p t", c=KC, h=2)
    x_in = nc.dram_tensor("x", RL, F32, kind="ExternalInput")
    y_out = nc.dram_tensor("y", RL, F32, kind="ExternalOutput")
    fw_ = [(ext(f"wg{k}", [NSCW * 64, 4096]), ext(f"wu{k}", [NSCW * 64, 4096]), ext(f"wd{k}", [2 * KC * 16, NFC * 128]))
           for k in range(4)]
    lnw = [ext(f"lnw{i}", [128, KC]) for i in range(6)]
    lnb = [ext(f"lnb{i}", [128, KC]) for i in range(6)]
    wmix_ab = ext("wmix_ab", [28, 128, 4096]); cos2 = ext("cos2", [128, SEQ]); sin2 = ext("sin2", [128, SEQ])
    hconst_ab = ext("hconst_ab", [4, 128, HCW]); perm = ext("perm", [128, 128]); ident = ext("ident", [128, 128])
    band = ext("band", [128, 256]); gnw = ext("gnw", [128, 4]); wo_ab = ext("wo_ab", [8, 128, 4096])
    wmix_dn = ext("wmix_dn", [32, 128, 4096]); wab = ext("wab", [128, KC, 16]); hconst_dn = ext("hconst_dn", [8, 128, DCW])
    mats = ext("mats", [6, 128, 128]); wo_dn = ext("wo_dn", [8, 128, 4096])
    Xs = [nc.dram_tensor(f"X{i}", RL, F32) for i in range(5)]
    XG = nc.dram_tensor("XG", [KC * 2 * 4 * 128, 1024], F32)
    Yd = nc.dram_tensor("Yd", [8, 128, SEQ], F32)
    Pd = nc.dram_tensor("Pd", [KC * 4 * 128, 2048], F32)
    Md = nc.dram_tensor("Md", [KC * 128, 2048], F32)
    gb = {"wg": (nc.dram_tensor("wg_b", [NSCW * 64, 4096], F32), nc.dram_tensor("wg_g", [NSCW * 256, 4096], F32)),
          "wu": (nc.dram_tensor("wu_b", [NSCW * 64, 4096], F32), nc.dram_tensor("wu_g", [NSCW * 256, 4096], F32)),
          "wd": (nc.dram_tensor("wd_b", [2 * KC * 16, NFC * 128], F32), nc.dram_tensor("wd_g", [2 * KC * 64, NFC * 128], F32))}
    XGv = XG.ap().rearrange("(c h q p) t -> c h q p t", c=KC, h=2, q=4)
    Yv = Yd.ap()
    Pv = Pd.ap().rearrange("(c q p) t -> c q p t", c=KC, q=4)
    Mv = Md.ap().rearrange("(c p) t -> c p t", c=KC)

    with ExitStack() as es0:
        S = Sched(nc, es0)
        S.out_res = S.res("out"); S.out_res.keep = True
        CC = S.res("cc"); CC.keep = True

        def ffn_stage(k, Xin, Xout, li):
            with ExitStack() as es:
                A = ffn_alloc(S, nc, es, NFC, sfx=f"_{k}")
                ws = WStream(S, nc, es, 5, name=f"wr{k}")
                g1 = Gather(S, nc, f"wg{k}", fw_[k][0], NSCW, 64, 4096, CC, bufs=gb["wg"])
                g2 = Gather(S, nc, f"wu{k}", fw_[k][1], NSCW, 64, 4096, CC, bufs=gb["wu"])
                g3 = Gather(S, nc, f"wd{k}", fw_[k][2], 2 * KC, 16, NFC * 128, CC, bufs=gb["wd"])
                emit_ffn_ln(S, nc, A, ws, Xin, Xout, g1, g2, g3, lnw[li], lnb[li], NT, NFC)
                S.barrier_all(); S.flush()

        def exch_ag(X):
            emit_allgather_x(S, nc, X, XG, CC)
            S.barrier_all(); S.flush()

        def outproj_rs(wo_d, sfx):
            with ExitStack() as es:
                emit_outproj(S, nc, es, Yv, wo_d, Pv, sfx=sfx)
                S.barrier_all(); S.flush()
            emit_reducescatter(S, nc, Pd, Md, CC)
            S.barrier_all(); S.flush()

        def resid_ln(Xin, Xout, li):
            with ExitStack() as es:
                emit_resid_ln(S, nc, es, Xin, Mv, Xout, lnw[li], lnb[li], NT, sfx=f'_{li}')
                S.barrier_all(); S.flush()

        X = [rlv(t) for t in Xs]
        DBG = S.res("dbg"); DBG.keep = True

        def dump(name, src_ap, shape):
            if not debug:
                return
            o = nc.dram_tensor(name, shape, F32, kind="ExternalOutput").ap()
            S.op("sp", lambda E: E.dma_start(out=o, in_=src_ap), wr=[DBG], dma=DBG)
            S.barrier_all(); S.flush()
        ffn_stage(0, rlv(x_in), X[0], 0)
        exch_ag(Xs[0])
        dump('dump_XG', XG.ap()[0:1024, :], [1024, 1024])
        with ExitStack() as es:
            emit_ret(S, nc, es, XGv, wmix_ab, cos2, sin2, hconst_ab, perm, ident, gnw, Yv, 4)
            S.barrier_all(); S.flush()
        with ExitStack() as es:
            emit_swa(S, nc, es, XGv, wmix_ab, 16, band, ident, Yv, 4)
            S.barrier_all(); S.flush()
        dump('dump_Y', Yd.ap(), [8, 128, SEQ])
        outproj_rs(wo_ab, '_a')
        dump('dump_P', Pd.ap()[0:512, :], [512, 2048])
        dump('dump_M', Md.ap(), [KC * 128, 2048])
        resid_ln(X[0], X[1], 1)
        ffn_stage(1, X[1], X[2], 2)
        ffn_stage(2, X[2], X[3], 3)
        exch_ag(Xs[3])
        with ExitStack() as es:
            emit_gdn(S, nc, es, XGv, wmix_dn, wab, hconst_dn, mats, ident, Yv, 4)
            S.barrier_all(); S.flush()
        outproj_rs(wo_dn, '_d')
        resid_ln(X[3], X[4], 4)
        ffn_stage(3, X[4], rlv(y_out), 5)
        for i in range(5):
            dump(f'dump_X{i}', Xs[i].ap(), RL)
    return nc


def _prep_w_in(W):
    Dm, F = W.shape
    return W.reshape(Dm // 128, 128, F // 128, 128).transpose(2, 1, 0, 3).reshape(F // 128, 128, (Dm // 128) * 128)


def _prep_w_out(W):
    F, Dm = W.shape
    return W.reshape(F // 128, 128, Dm // 128, 128).transpose(2, 1, 0, 3).reshape(Dm // 128, 128, (F // 128) * 128)


def _shard_in(wp, r):
    n = wp.shape[0]
    return np.ascontiguousarray(wp.reshape(n // 2, 4, 64, 4096)[:, r].reshape((n // 2) * 64, 4096))


def _shard_out(wp, r):
    return np.ascontiguousarray(wp.reshape(2 * KC, 4, 16, -1)[:, r].reshape(2 * KC * 16, -1))


def _prep_chunk(Wc):
    return np.ascontiguousarray(Wc.reshape(32, 128, 128).transpose(1, 0, 2).reshape(128, 4096))


def _vec(v):
    return np.ascontiguousarray(np.asarray(v, np.float32).reshape(-1, 128).T)


_PERMIDX = np.concatenate([np.arange(0, 128, 2), np.arange(1, 128, 2)])


def _const_tables():
    m = {}
    pos = np.arange(SEQ, dtype=np.float32)
    inv_freq = (1.0 / (np.float32(10000.0) ** np.linspace(0.0, 1.0, 64, dtype=np.float32))).astype(np.float32)
    ang = (pos[None, :] * inv_freq[:, None]).astype(np.float32)
    m["cos2"] = np.concatenate([np.cos(ang), np.cos(ang)], 0).astype(np.float32)
    m["sin2"] = np.concatenate([-np.sin(ang), np.sin(ang)], 0).astype(np.float32)
    p = np.zeros((128, 128), np.float32); p[(np.arange(128) + 64) % 128, np.arange(128)] = 1
    m["perm"] = p
    m["ident"] = np.eye(128, dtype=np.float32)
    d = np.arange(256)[None, :] - np.arange(128)[:, None]
    m["band"] = ((d >= 0) & (d <= 128)).astype(np.float32)
    idx = np.arange(128); same = (idx[:, None] // 64) == (idx[None, :] // 64)
    U = (same & (idx[:, None] <= idx[None, :])).astype(np.float32)
    low = same & (idx[:, None] >= idx[None, :])
    NEGLOW = np.where(low, 0.0, -1e5).astype(np.float32)
    STRICT = (same & (idx[:, None] > idx[None, :])).astype(np.float32)
    m["mats"] = np.stack([U, same.astype(np.float32), NEGLOW, np.ascontiguousarray(NEGLOW.T), STRICT, np.eye(128, dtype=np.float32)])
    return m


def _ret_head_consts(g):
    hc = np.zeros((4, 128, HCW), np.float32)
    idx = np.arange(128, dtype=np.float32)
    for a in range(4):
        h = 4 * g + a
        lg = np.log1p(-np.exp2(np.float32(-5.0 - h))).astype(np.float32)
        hc[a, :, 0:512] = np.tile(np.exp(lg * (idx + 1)), 4)[None, :]
        rel = idx[None, :] - idx[:, None]
        hc[a, :, 512:640] = np.where(rel >= 0, np.exp(lg * np.maximum(rel, 0)), 0)
        hc[a, :, 640] = np.exp(lg * (127 - idx))
        hc[a, :, 641] = np.exp(lg * 128)
    return hc


def kernel(x, ffn_w_gate, ffn_w_up, ffn_w_down, ln_w, ln_b, ab_w_in, ab_gn_w, ab_w_out,
           dn_w_in, dn_conv_w, dn_a_log, dn_dt_bias, dn_norm_w, dn_w_out):
    global _PROG
    from concourse.bass_utils import run_bass_kernel_spmd
    if _PROG is None:
        _PROG = _build()
    in_maps = _prepare(x, ffn_w_gate, ffn_w_up, ffn_w_down, ln_w, ln_b, ab_w_in, ab_gn_w, ab_w_out,
                       dn_w_in, dn_conv_w, dn_a_log, dn_dt_bias, dn_norm_w, dn_w_out)
    res = run_bass_kernel_spmd(_PROG, in_maps, core_ids=list(range(NCORE)))
    return _assemble(res.results, np.asarray(x).shape)


def _assemble(results, shape):
    B, Sq, Dm = shape
    outs = []
    for r in results:
        yc = np.asarray(r["y"], np.float32).reshape(KC, 2, 128, 1024).transpose(1, 3, 0, 2).reshape(TOK_PER_CORE, Dm)
        outs.append(yc)
    return np.ascontiguousarray(np.concatenate(outs, axis=0).reshape(B, Sq, Dm).astype(np.float32))


def _prepare(x, ffn_w_gate, ffn_w_up, ffn_w_down, ln_w, ln_b, ab_w_in, ab_gn_w, ab_w_out,
             dn_w_in, dn_conv_w, dn_a_log, dn_dt_bias, dn_norm_w, dn_w_out):
    f32 = np.float32
    x = np.asarray(x, f32)
    B, Sq, Dm = x.shape
    xf = x.reshape(B * Sq, Dm)
    common = _const_tables()
    for i, (l, s) in enumerate(((0, 0), (0, 1), (0, 2), (1, 0), (1, 1), (1, 2))):
        common[f"lnw{i}"] = _vec(ln_w[l, s]); common[f"lnb{i}"] = _vec(ln_b[l, s])
    ffn_sh = {r: {} for r in range(4)}
    for k, (l, s) in enumerate(((0, 0), (0, 1), (1, 0), (1, 1))):
        wgp = _prep_w_in(np.asarray(ffn_w_gate[l, s], f32)); wup = _prep_w_in(np.asarray(ffn_w_up[l, s], f32))
        wdp = _prep_w_out(np.asarray(ffn_w_down[l, s], f32))
        for r in range(4):
            ffn_sh[r][f"wg{k}"] = _shard_in(wgp, r); ffn_sh[r][f"wu{k}"] = _shard_in(wup, r); ffn_sh[r][f"wd{k}"] = _shard_out(wdp, r)
        del wgp, wup, wdp
    abw = np.asarray(ab_w_in[0], f32); abo = np.asarray(ab_w_out[0], f32); gn = np.asarray(ab_gn_w[0], f32)
    dnw = np.asarray(dn_w_in[0], f32); dno = np.asarray(dn_w_out[0], f32); cw = np.asarray(dn_conv_w[0], f32)
    alog = np.asarray(dn_a_log[0], f32); dtb = np.asarray(dn_dt_bias[0], f32); nw = np.asarray(dn_norm_w[0], f32)
    grp = {}
    for g in range(4):
        m = {}
        ch = []
        for a in range(4):
            h = 4 * g + a
            ch.append(_prep_chunk(abw[:, h * 128:(h + 1) * 128][:, _PERMIDX]))
            ch.append(_prep_chunk(abw[:, 2048 + h * 128:2048 + (h + 1) * 128][:, _PERMIDX]))
            ch.append(_prep_chunk(abw[:, 4096 + h * 128:4096 + (h + 1) * 128]))
            ch.append(_prep_chunk(abw[:, 6144 + h * 128:6144 + (h + 1) * 128]))
        for a in range(4):
            h = 4 * g + a
            for o in (8192, 10240, 12288):
                ch.append(_prep_chunk(abw[:, o + h * 128:o + (h + 1) * 128]))
        m["wmix_ab"] = np.stack(ch)
        m["hconst_ab"] = _ret_head_consts(g)
        m["gnw"] = np.ascontiguousarray(gn.reshape(16, 128)[4 * g:4 * g + 4].T)
        rows = [abo[(4 * g + a) * 128:(4 * g + a + 1) * 128] for a in range(4)] + \
               [abo[2048 + (4 * g + a) * 128:2048 + (4 * g + a + 1) * 128] for a in range(4)]
        m["wo_ab"] = np.ascontiguousarray(np.stack(rows))
        ch = []
        for a in range(8):
            h = 8 * g + a
            for o in (0, 4096, 8192, 12288):
                ch.append(_prep_chunk(dnw[:, o + h * 128:o + (h + 1) * 128]))
        m["wmix_dn"] = np.stack(ch)
        colsel = [16384 + 8 * g + a for a in range(8)] + [16416 + 8 * g + a for a in range(8)]
        m["wab"] = np.ascontiguousarray(dnw[:, colsel].reshape(32, 128, 16).transpose(1, 0, 2))
        hc = np.zeros((8, 128, DCW), f32)
        for a in range(8):
            h = 8 * g + a
            for k, o in enumerate((0, 4096, 8192)):
                hc[a, :, 4 * k:4 * k + 4] = cw[:, o + h * 128:o + (h + 1) * 128].T
            hc[a, :, 12] = alog[h]; hc[a, :, 13] = dtb[h]; hc[a, :, 14] = nw
        m["hconst_dn"] = hc
        m["wo_dn"] = np.ascontiguousarray(dno[8 * g * 128:(8 * g + 8) * 128].reshape(8, 128, 4096))
        grp[g] = m
    in_maps = []
    for c in range(NCORE):
        xc = xf[c * TOK_PER_CORE:(c + 1) * TOK_PER_CORE]
        m = {"x": np.ascontiguousarray(xc.reshape(2, 1024, KC, 128).transpose(2, 0, 3, 1).reshape(KC * 2 * 128, 1024))}
        m.update(common); m.update(ffn_sh[c % 4]); m.update(grp[c % 4])
        in_maps.append(m)
    return in_maps
```

```python
import numpy as np
from contextlib import ExitStack
import concourse.bass as bass
import concourse.mybir as mybir

F32, BF16 = mybir.dt.float32, mybir.dt.bfloat16
AF = mybir.ActivationFunctionType
ALU = mybir.AluOpType


class Res:
    __slots__ = ("name", "w", "rd", "rd_dma", "sem", "cnt", "alias", "keep", "qt")

    def __init__(self, name):
        self.name = name
        self.w = None
        self.rd = {}
        self.rd_dma = []
        self.sem = None
        self.cnt = 0
        self.alias = []
        self.keep = False
        self.qt = None


def alias(a, b):
    a.alias.append(b)
    b.alias.append(a)


class Op:
    __slots__ = ("eng", "fn", "deps", "dma", "sig", "val", "idx", "cc")


class Sched:
    def __init__(self, nc, es):
        self.nc = nc
        self.es = es
        self.ops = []
        self.engs = {"pe": nc.tensor, "act": nc.scalar, "dve": nc.vector, "pool": nc.gpsimd, "sp": nc.sync}
        self.esem = {}
        self.nsem = 0
        self.allres = []
        self.cnt = {}
        self.seen = {e: {} for e in self.engs}
        self.last = {}
        self.dmas = []
        self.nwait = 0
        self.nops = 0
        self.nid = 0
        self.free_sems = {'sw': [], 'hw': []}

    def res(self, name):
        r = Res(name)
        self.allres.append(r)
        return r

    def barrier_all(self):
        pend = [o for o in self.last.values()] + list(self.dmas)
        for e in self.engs:
            o = Op()
            o.cc, o.eng, o.fn, o.dma, o.sig, o.val = False, e, None, None, False, None
            self.nid += 1
            o.idx = self.nid
            o.deps = list(pend)
            for d in pend:
                d.sig = True
            self.ops.append(o)
        for r in self.allres:
            r.w, r.rd, r.rd_dma = None, {}, []
        self.last = {}
        self.dmas = []

    def op(self, eng, fn, rd=(), wr=(), dma=None, cc=False):
        o = Op()
        o.cc = cc
        o.eng, o.fn, o.dma, o.sig, o.val = eng, fn, dma, False, None
        self.nid += 1
        o.idx = self.nid
        deps = {}
        rds, wrs = [], []
        for r in rd:
            rds.append(r)
            rds.extend(r.alias)
        for w in wr:
            wrs.append(w)
            wrs.extend(w.alias)
        for r in rds:
            if r.w is not None:
                self._dep(o, r.w, deps, raw=True)
        for w in wrs:
            if w.w is not None:
                self._dep(o, w.w, deps, raw=False)
            for x in w.rd.values():
                self._dep(o, x, deps, raw=False)
            for x in w.rd_dma:
                self._dep(o, x, deps, raw=False)
        o.deps = list(deps.values())
        for d in o.deps:
            d.sig = True
        for r in rds:
            if o.dma is not None:
                r.rd_dma.append(o)
            else:
                r.rd[eng] = o
        for w in wrs:
            w.w = o
            w.rd = {}
            w.rd_dma = []
        self.ops.append(o)
        if fn is not None:
            if o.dma is not None:
                self.dmas.append(o)
            else:
                self.last[eng] = o
        return o

    def _dep(self, o, d, deps, raw):
        if d is o:
            return
        if d.dma is None and o.dma is None and d.eng == o.eng:
            if o.eng == "pe":
                return
        deps[d.idx] = d

    def barrier_wait(self, eng, reslist):
        return self.op(eng, None, rd=reslist)

    def _newsem(self, name):
        self.nsem += 1
        return self.es.enter_context(self.nc.semaphore(f"{name}_{self.nsem}"))

    def flush(self):
        nc = self.nc
        cnt, seen = self.cnt, self.seen
        if not self.esem:
            for e in self.engs:
                self.esem[e] = self._newsem("e_" + e)
                cnt[e] = 0
        nwait = 0
        used = []
        for o in self.ops:
            E = self.engs[o.eng]
            sn = seen[o.eng]
            for d in o.deps:
                if d.dma is not None:
                    sem = d.dma.sem
                    val = d.val if d.cc else 16 * d.dma.cnt
                else:
                    sem = self.esem[d.eng]
                    val = d.val
                k = id(sem)
                if sn.get(k, 0) >= val:
                    continue
                sn[k] = val
                E.wait_ge(sem, val)
                nwait += 1
            if o.fn is None:
                continue
            ins = o.fn(E)
            if o.dma is not None:
                r = o.dma
                qt = "cc" if o.cc else ("sw" if o.eng == "pool" else "hw")
                assert r.qt in (None, qt), (r.name, r.qt, qt)
                r.qt = qt
                if r.sem is None:
                    if qt != "cc" and self.free_sems[qt]:
                        r.sem, r.cnt = self.free_sems[qt].pop()
                    else:
                        r.sem = self._newsem("d_" + r.name)
                    used.append(r)
                r.cnt += 1
                if o.cc:
                    o.val = r.cnt
                    ins.then_inc(r.sem)
                else:
                    ins.then_inc(r.sem, 16)
            elif o.sig:
                cnt[o.eng] += 1
                o.val = cnt[o.eng]
                ins.then_inc(self.esem[o.eng], 1)
        for r in used:
            if not r.keep and r.qt != "cc":
                self.free_sems[r.qt].append((r.sem, r.cnt))
                r.sem = None
        self.allres = [r for r in self.allres if r.keep]
        self.nwait += nwait
        self.nops += len(self.ops)
        self.stats = dict(n_ops=self.nops, n_wait=self.nwait, n_sem=self.nsem, sig=dict(cnt))
        self.ops = []


class Gather:
    def __init__(self, S, nc, name, ext, nsc, rpr, cols, ccres, nr=4, bufs=None):
        self.S, self.nc = S, nc
        self.nsc, self.rpr, self.cols, self.nr = nsc, rpr, cols, nr
        if bufs is None:
            bufs = (nc.dram_tensor(name + "_b", [nsc * rpr, cols], F32), nc.dram_tensor(name + "_g", [nsc * nr * rpr, cols], F32))
        self.b, self.g = bufs
        self.B = S.res(name + "_B")
        self.G = [S.res(f"{name}_G{i}") for i in range(nsc)]
        self.cc = ccres
        self.done = 0
        bb = self.b.ap()
        S.op("sp", lambda E: E.dma_start(out=bb, in_=ext), wr=[self.B], dma=self.B)

    def ensure(self, sc):
        sc = min(sc, self.nsc - 1)
        while self.done <= sc:
            i = self.done
            src = self.b.ap()[i * self.rpr:(i + 1) * self.rpr, :]
            dst = self.g.ap()[i * self.nr * self.rpr:(i + 1) * self.nr * self.rpr, :]
            self.S.op("pool", lambda E, src=src, dst=dst: E.collective_compute(
                "AllGather", ALU.bypass, replica_groups=[[0, 1, 2, 3], [4, 5, 6, 7]], ins=[src.opt()], outs=[dst.opt()]),
                rd=[self.B], wr=[self.G[i]], dma=self.cc, cc=True)
            self.done += 1

    def rows(self, sc, r0, r1):
        base = sc * self.nr * self.rpr
        return self.g.ap()[base + r0:base + r1, :]

KC = 32
TT = 512
LN_EPS = 1e-5
ALPHA = 4.0 ** 0.25


class Prefetcher:
    def __init__(self, S, gathers, depth=8):
        self.S, self.depth, self.pos = S, depth, 0
        self.q = [(g, i) for g in gathers for i in range(g.nsc)]

    def step(self, n=1):
        for _ in range(n):
            if self.pos >= len(self.q):
                return
            g, i = self.q[self.pos]
            if self.pos >= self.depth:
                g0, i0 = self.q[self.pos - self.depth]
                self.S.op("pool", None, rd=[g0.G[i0]])
            g.ensure(i)
            self.pos += 1

    def finish(self):
        self.step(len(self.q))


class WStream:
    def __init__(self, S, nc, es, nslot, name="wr"):
        self.S = S
        self.n = nslot
        self.buf = es.enter_context(nc.sbuf_tensor(name, [128, nslot, 4096], BF16))
        self.res = [S.res(f"{name}{i}") for i in range(nslot)]
        self.pending = []
        self.issued = 0
        self.consumed = 0

    def push(self, src_ap, ncols, gath=None, scs=(), la=4):
        self.pending.append((src_ap, ncols, gath, scs, la))

    def _issue(self, i):
        src, ncols, gath, scs, la = self.pending[i]
        if getattr(self.S, "pf", None) is not None:
            self.S.pf.step(1)
        s = i % self.n
        dst = self.buf[:, s, 0:ncols]
        rd = []
        if gath is not None:
            gath.ensure(max(scs) + la)
            rd = [gath.G[k] for k in scs]
        self.S.op("pool", lambda E, dst=dst, src=src: E.dma_start(out=dst, in_=src),
                  rd=rd, wr=[self.res[s]], dma=self.res[s])

    def next(self):
        j = self.consumed
        while self.issued < min(len(self.pending), j + self.n):
            self._issue(self.issued)
            self.issued += 1
        self.consumed += 1
        s = j % self.n
        return self.buf[:, s, :], self.res[s]


def ffn_alloc(S, nc, es, NFC, sfx=""):
    A = {}
    _n = nc.sbuf_tensor
    A["U1"] = es.enter_context(nc.sbuf_tensor("U1" + sfx, [128, KC * TT], F32))
    A["HT"] = es.enter_context(nc.sbuf_tensor("HT" + sfx, [128, NFC * TT], BF16))
    A["x32"] = es.enter_context(nc.sbuf_tensor("x32" + sfx, [128, 2, TT], F32))
    A["ones"] = es.enter_context(nc.sbuf_tensor("ones" + sfx, [128, 128], F32))
    A["lnw"] = es.enter_context(nc.sbuf_tensor("lnw_sb" + sfx, [128, KC], F32))
    A["lnb"] = es.enter_context(nc.sbuf_tensor("lnb_sb" + sfx, [128, KC], F32))
    if NFC * TT * 2 < 10 * TT * 4:
        A["LTb"] = es.enter_context(nc.sbuf_tensor("LTb" + sfx, [128, 10, TT], F32))
    A["ps"] = [es.enter_context(nc.psum_tensor(f"ps{i}" + sfx, [128, TT], F32)) for i in range(8)]
    A["psr"] = [S.res(f"ps{i}") for i in range(8)]
    return A


def rl_all(X, t0):
    hf, o = t0 // 1024, t0 % 1024
    return X[:, hf, :, o:o + TT].rearrange("c p t -> p c t")


def rl_one(X, dc, t0):
    hf, o = t0 // 1024, t0 % 1024
    return X[dc, hf, :, o:o + TT]


def emit_ffn_ln(S, nc, A, ws, x_dram, y_dram, wg, wu, wd, lnw, lnb, NT, NFC, first=True):
    U1 = A["U1"]
    r = U1[:].rearrange("p (c t) -> p c t", c=KC)
    U1b = U1.bitcast(BF16)
    xb = U1b[:, 0:KC * TT].rearrange("p (c t) -> p c t", c=KC)
    sil = [r[:, 16, :], r[:, 17, :]]
    HT2 = A["HT"]
    HT = HT2[:].rearrange("p (f t) -> p f t", f=NFC)
    ps, psr = A["ps"], A["psr"]
    ones = A["ones"]
    XB = S.res("XB")
    R = [S.res(f"R{d}") for d in range(KC)]
    for d in range(16):
        alias(R[d], XB)
    H = [S.res(f"H{f}") for f in range(NFC)]
    X32 = [S.res("x32a"), S.res("x32b")]
    x32 = A["x32"]
    LT = [S.res(f"LT{k}") for k in range(10)]
    if "LTb" in A:
        lt = [A["LTb"][:, k, :] for k in range(10)]
    else:
        HTf = HT2.bitcast(F32)
        lt = [HTf[:, k * TT:(k + 1) * TT] for k in range(10)]
        for k in range(10):
            alias(LT[k], H[2 * k])
            alias(LT[k], H[2 * k + 1])
    CONST = S.res("const")
    if first:
        S.op("dve", lambda E: E.memset(ones[:], 1.0), wr=[CONST])
    LNP = S.res("lnp")
    S.op("sp", lambda E: E.dma_start(out=A["lnw"][:], in_=lnw), wr=[LNP], dma=LNP)
    S.op("sp", lambda E: E.dma_start(out=A["lnb"][:], in_=lnb), wr=[LNP], dma=LNP)

    npc = -(-NFC // 32)
    bnd = [round(i * NFC / npc) for i in range(npc + 1)]
    thirds = [(bnd[i], bnd[i + 1]) for i in range(npc)]
    for ti in range(NT):
        for fc in range(NFC):
            ws.push(wg.rows(fc // 2, (fc % 2) * 128, (fc % 2 + 1) * 128), 4096, wg, (fc // 2,))
            ws.push(wu.rows(fc // 2, (fc % 2) * 128, (fc % 2 + 1) * 128), 4096, wu, (fc // 2,))
        for dc in range(KC):
            for (a, b) in thirds:
                ws.push(wd.g.ap()[dc * 128:(dc + 1) * 128, a * 128:b * 128], (b - a) * 128, wd, (2 * dc, 2 * dc + 1))

    for ti in range(NT):
        t0 = ti * TT
        src = rl_all(x_dram, t0)
        S.op("pool", lambda E, src=src: E.dma_start(out=xb, in_=src), wr=[XB], dma=XB)
        for fc in range(NFC):
            b = fc % 2
            wgs, wgr = ws.next()
            pg, pgr = ps[b], psr[b]
            for kc in range(KC):
                S.op("pe", lambda E, o=pg, w=wgs[:, kc * 128:(kc + 1) * 128], x=xb[:, kc, :], kc=kc:
                     E.matmul(o[:], w, x, start=(kc == 0), stop=(kc == KC - 1)), rd=[wgr, XB], wr=[pgr])
            wus, wur = ws.next()
            pu, pur = ps[2 + b], psr[2 + b]
            for kc in range(KC):
                S.op("pe", lambda E, o=pu, w=wus[:, kc * 128:(kc + 1) * 128], x=xb[:, kc, :], kc=kc:
                     E.matmul(o[:], w, x, start=(kc == 0), stop=(kc == KC - 1)), rd=[wur, XB], wr=[pur])
            S.op("act", lambda E, o=sil[b], i=pg: E.activation(out=o, in_=i[:], func=AF.Silu),
                 rd=[pgr], wr=[R[16 + b]])
            S.op("dve", lambda E, o=HT[:, fc, :], a=sil[b], i=pu: E.tensor_tensor(out=o, in0=a, in1=i[:], op=ALU.mult),
                 rd=[R[16 + b], pur], wr=[H[fc]])
        for dc in range(KC):
            b = dc % 2
            po, por = ps[4 + b], psr[4 + b]
            xsrc = rl_one(x_dram, dc, t0)
            S.op("sp", lambda E, o=x32[:, b, :], i=xsrc: E.dma_start(out=o, in_=i), wr=[X32[b]], dma=X32[b])
            S.op("act", lambda E, o=x32[:, b, :]: E.activation(out=o, in_=o, func=AF.Copy, scale=float(ALPHA)),
                 rd=[X32[b]], wr=[X32[b]])
            for (a, bb) in thirds:
                wds, wdr = ws.next()
                for fc in range(a, bb):
                    S.op("pe", lambda E, o=po, w=wds[:, (fc - a) * 128:(fc - a + 1) * 128], h=HT[:, fc, :], fc=fc:
                         E.matmul(o[:], w, h, start=(fc == 0), stop=(fc == NFC - 1)), rd=[wdr, H[fc]], wr=[por])
            S.op("dve", lambda E, o=r[:, dc, :], i=po, x=x32[:, b, :]:
                 E.scalar_tensor_tensor(out=o, in0=i[:], scalar=0.5, in1=x, op0=ALU.mult, op1=ALU.add),
                 rd=[por, X32[b]], wr=[R[dc]])
        emit_ln(S, nc, A, r, R, lt, LT, ps, psr, LNP, CONST, y_dram, t0)


def emit_ln(S, nc, A, r, R, lt, LT, ps, psr, LNP, CONST, y_dram, t0, store_extra=None):
    ones = A["ones"]
    p1, p1r, p2, p2r = ps[6], psr[6], ps[7], psr[7]
    for dc in range(KC):
        b = dc % 2
        S.op("pe", lambda E, x=r[:, dc, :], dc=dc: E.matmul(p1[:], ones[:], x, start=(dc == 0), stop=(dc == KC - 1)),
             rd=[R[dc], CONST], wr=[p1r])
        S.op("act", lambda E, o=lt[b], x=r[:, dc, :]: E.activation(out=o, in_=x, func=AF.Square), rd=[R[dc]], wr=[LT[b]])
        S.op("pe", lambda E, x=lt[b], dc=dc: E.matmul(p2[:], ones[:], x, start=(dc == 0), stop=(dc == KC - 1)),
             rd=[LT[b], CONST], wr=[p2r])
    mean, msq, var, rstd = lt[2], lt[3], lt[4], lt[5]
    inv = 1.0 / (KC * 128)
    S.op("dve", lambda E: E.tensor_scalar(out=mean, in0=p1[:], scalar1=inv, scalar2=None, op0=ALU.mult), rd=[p1r], wr=[LT[2]])
    S.op("dve", lambda E: E.tensor_tensor(out=msq, in0=mean, in1=mean, op=ALU.mult), rd=[LT[2]], wr=[LT[3]])
    S.op("dve", lambda E: E.scalar_tensor_tensor(out=var, in0=p2[:], scalar=inv, in1=msq, op0=ALU.mult, op1=ALU.subtract),
         rd=[p2r, LT[3]], wr=[LT[4]])
    S.op("dve", lambda E: E.tensor_scalar_add(var, var, LN_EPS), rd=[LT[4]], wr=[LT[4]])
    S.op("act", lambda E: E.activation(out=var, in_=var, func=AF.Sqrt), rd=[LT[4]], wr=[LT[4]])
    S.op("dve", lambda E: E.reciprocal(out=rstd, in_=var), rd=[LT[4]], wr=[LT[5]])
    for dc in range(KC):
        b = dc % 2
        t, y = lt[6 + b], lt[8 + b]
        S.op("dve", lambda E, t=t, x=r[:, dc, :]: E.tensor_tensor(out=t, in0=x, in1=mean, op=ALU.subtract),
             rd=[R[dc], LT[2]], wr=[LT[6 + b]])
        S.op("dve", lambda E, t=t: E.tensor_tensor(out=t, in0=t, in1=rstd, op=ALU.mult), rd=[LT[6 + b], LT[5]], wr=[LT[6 + b]])
        S.op("act", lambda E, t=t, y=y, dc=dc: E.activation(out=y, in_=t, func=AF.Identity,
                                                          bias=A["lnb"][:, dc:dc + 1], scale=A["lnw"][:, dc:dc + 1]),
             rd=[LT[6 + b], LNP], wr=[LT[8 + b]])
        dst = rl_one(y_dram, dc, t0)
        S.op("sp", lambda E, y=y, dst=dst: E.dma_start(out=dst, in_=y), rd=[LT[8 + b]], wr=[S.out_res], dma=LT[8 + b])

HD = 128
NRH = 4
NAH = 4
SWA_PAT = ((128, 1), (512, 4), (2048, 16))
HCW = 512 + 128 + 2


def _mm(S, out, lhsT, rhs, rd, wr, start=True, stop=True):
    return S.op("pe", lambda E: E.matmul(out, lhsT, rhs, start=start, stop=stop), rd=rd, wr=wr)


def _act(S, out, in_, func, rd, wr, **kw):
    return S.op("act", lambda E: E.activation(out=out, in_=in_, func=func, **kw), rd=rd, wr=wr)


def _tt(S, out, a, b, op, rd, wr):
    return S.op("dve", lambda E: E.tensor_tensor(out=out, in0=a, in1=b, op=op), rd=rd, wr=wr)


def _stt(S, out, a, sc, b, op0, op1, rd, wr):
    return S.op("dve", lambda E: E.scalar_tensor_tensor(out=out, in0=a, scalar=sc, in1=b, op0=op0, op1=op1), rd=rd, wr=wr)


def _ts(S, out, a, s1, op0, rd, wr, s2=None, op1=None):
    if op1 is None:
        return S.op("dve", lambda E: E.tensor_scalar(out=out, in0=a, scalar1=s1, scalar2=None, op0=op0), rd=rd, wr=wr)
    return S.op("dve", lambda E: E.tensor_scalar(out=out, in0=a, scalar1=s1, scalar2=s2, op0=op0, op1=op1), rd=rd, wr=wr)


def x_tile_src(X1G, ti):
    q, hf, j = ti // 4, (ti % 4) // 2, ti % 2
    return X1G[:, hf, q, :, j * TT:(j + 1) * TT].rearrange("c p t -> p c t")


def emit_ret(S, nc, es, X1G, wmix, cos2, sin2, hconst, perm_d, ident_d, gnw_d, Y, NQ, dbg=9, nrh=NRH):
    SEQ = NQ * 2048
    NTI = SEQ // TT
    sb = lambda name, shape, dt: es.enter_context(nc.sbuf_tensor(name, shape, dt))
    pp = lambda name, shape, dt=F32: es.enter_context(nc.psum_tensor(name, shape, dt))
    R = S.res
    xb = sb("m_xb", [128, KC, TT], BF16); XB = R("m_xb")
    wbuf = sb("m_w", [128, 4, 4096], BF16); WB = [R(f"m_w{i}") for i in range(4)]
    ones = sb("m_ones", [128, 128], F32); onesb = sb("m_onesb", [128, 128], BF16)
    perm = sb("m_perm", [128, 128], F32)
    ident = sb("m_ident", [128, 128], BF16)
    gnw = sb("m_gnw", [128, NRH], F32)
    hc = sb("m_hc", [128, HCW], F32); HC = R("m_hc")
    CONST = R("m_const")
    S.op("dve", lambda E: E.memset(ones[:], 1.0), wr=[CONST])
    S.op("dve", lambda E: E.memset(onesb[:], 1.0), wr=[CONST])
    S.op("sp", lambda E: E.dma_start(out=perm[:], in_=perm_d), wr=[CONST], dma=CONST)
    CONSTI = R("m_consti")
    S.op("pool", lambda E: E.dma_start(out=ident[:], in_=ident_d), wr=[CONSTI], dma=CONSTI)
    S.op("sp", lambda E: E.dma_start(out=gnw[:], in_=gnw_d), wr=[CONST], dma=CONST)
    pA = [pp("m_pA0", [128, TT]), pp("m_pA1", [128, TT])]; PA = [R("m_pA0"), R("m_pA1")]
    pB = pp("m_pB", [128, TT]); PB = R("m_pB")
    pS1, PS1 = pB, PB
    pS2 = pp("m_pS2", [128, TT]); PS2 = R("m_pS2")
    pV = pp("m_pV", [128, TT])[:, 0:128]; PV = R("m_pV")
    pK = pp("m_pK", [128, TT])[:, 0:256]; PK = R("m_pK")
    pS = pp("m_pS", [128, TT])[:, 0:256]; PS = R("m_pS")
    pO = pp("m_pO", [128, TT])[:, 0:256]; PO = R("m_pO")
    state = {"pa": 0}

    def load_w(n, base):
        for i in range(n):
            S.op("pool", lambda E, i=i: E.dma_start(out=wbuf[:, i, :], in_=wmix[base + i]), wr=[WB[i]], dma=WB[i])

    def load_x(ti):
        src = x_tile_src(X1G, ti)
        if getattr(S, "pf", None) is not None:
            S.pf.step(3)
        S.op("pool", lambda E: E.dma_start(out=xb[:], in_=src), wr=[XB], dma=XB)

    def proj_fm(wi):
        b = state["pa"]; state["pa"] ^= 1
        for kc in range(KC):
            _mm(S, pA[b][:], wbuf[:, wi, kc * 128:(kc + 1) * 128], xb[:, kc, :], [WB[wi], XB], [PA[b]],
                start=(kc == 0), stop=(kc == KC - 1))
        return pA[b], PA[b]

    qf = sb("r_qf", [128, TT], F32); QF = R("r_qf")
    t1 = sb("r_t1", [128, TT], F32); T1 = R("r_t1")
    t2 = sb("r_t2", [128, TT], F32); T2 = R("r_t2")
    cs = sb("r_cs", [128, 2, TT], F32); CS = R("r_cs")
    qr = sb("r_qr", [128, TT], BF16); QR = R("r_qr")
    qx = sb("r_qx", [128, TT], BF16); QX = R("r_qx")
    kr = sb("r_kr", [128, TT], BF16); KR = R("r_kr")
    kz = sb("r_kz", [128, 4, 128], BF16); KZ = [R(f"r_kz{c}") for c in range(4)]
    vt = sb("r_vt", [128, 4, 128], BF16); VT = [R(f"r_vt{c}") for c in range(4)]
    gs = sb("r_gs", [128, TT], F32); GS = R("r_gs")
    ptb = sb("r_pt", [128, 128], BF16); PTB = R("r_pt")
    ot = sb("r_o", [128, TT], F32); OT = R("r_o")
    sq = sb("r_sq", [128, TT], F32); SQ = R("r_sq")
    mean = sb("r_mean", [128, TT], F32); MEAN = R("r_mean")
    var = sb("r_var", [128, TT], F32); VAR = R("r_var")
    yb = sb("r_yb", [128, TT], F32); YB = R("r_yb")
    st = sb("r_st", [128, 128], F32); ST = R("r_st")
    stb = sb("r_stb", [128, 128], BF16); STB = R("r_stb")
    YD = R("Ydram")

    def rope(pin, PIN, dst, DST, extra=None, EXTRA=None, scale=1.0):
        _act(S, qf[:], pin[:], AF.Copy, [PIN], [QF], scale=float(scale))
        _mm(S, pB[:], perm[:], qf[:], [CONST, QF], [PB])
        _tt(S, t1[:], qf[:], cs[:, 0, :], ALU.mult, [QF, CS], [T1])
        _tt(S, t2[:], pB[:], cs[:, 1, :], ALU.mult, [PB, CS], [T2])
        _tt(S, dst[:], t1[:], t2[:], ALU.add, [T1, T2], [DST])
        if extra is not None:
            _tt(S, t1[:], t1[:], t2[:], ALU.add, [T1, T2], [T1])
            _tt(S, extra[:], t1[:], hc[:, 0:512], ALU.mult, [T1, HC], [EXTRA])

    for a in range(nrh):
        load_w(4, 4 * a)
        S.op("sp", lambda E, a=a: E.dma_start(out=hc[:], in_=hconst[a]), wr=[HC], dma=HC)
        S.op("dve", lambda E: E.memset(st[:], 0.0), wr=[ST])
        S.op("dve", lambda E: E.memset(stb[:], 0.0), wr=[STB])
        for ti in range(NTI):
            T0 = ti * TT
            load_x(ti)
            S.op("sp", lambda E, T0=T0: E.dma_start(out=cs[:, 0, :], in_=cos2[:, T0:T0 + TT]), wr=[CS], dma=CS)
            S.op("sp", lambda E, T0=T0: E.dma_start(out=cs[:, 1, :], in_=sin2[:, T0:T0 + TT]), wr=[CS], dma=CS)
            if dbg < 1:
                continue
            p, P = proj_fm(0)
            rope(p, P, qr, QR, qx, QX)
            p, P = proj_fm(1)
            rope(p, P, kr, KR, scale=HD ** -0.5)
            p, P = proj_fm(3)
            _act(S, gs[:], p[:], AF.Silu, [P], [GS])
            for c in range(4 if dbg >= 2 else 0):
                cs_ = slice(c * 128, (c + 1) * 128)
                for kc in range(KC):
                    _mm(S, pV, xb[:, kc, cs_], wbuf[:, 2, kc * 128:(kc + 1) * 128], [XB, WB[2]], [PV],
                        start=(kc == 0), stop=(kc == KC - 1))
                _act(S, vt[:, c, :], pV, AF.Copy, [PV], [VT[c]])
                _mm(S, pK[:, 0:128], kr[:, cs_], ident[:], [KR, CONSTI], [PK])
                _ts(S, kz[:, c, :], pK[:, 0:128], hc[:, 640:641], ALU.mult, [PK, HC], [KZ[c]])
                _mm(S, pS[:, 0:128], kr[:, cs_], qr[:, cs_], [KR, QR], [PS])
                _tt(S, ptb[:], pS[:, 0:128], hc[:, 512:640], ALU.mult, [PS, HC], [PTB])
                _mm(S, pO[:, 0:128], vt[:, c, :], ptb[:], [VT[c], PTB], [PO], start=True, stop=False)
                _mm(S, pO[:, 0:128], stb[:], qx[:, cs_], [STB, QX], [PO], start=False, stop=True)
                _act(S, ot[:, cs_], pO[:, 0:128], AF.Copy, [PO], [OT])
                _mm(S, pK[:, 128:256], kz[:, c, :], vt[:, c, :], [KZ[c], VT[c]], [PK])
                _stt(S, st[:], st[:], hc[:, 641:642], pK[:, 128:256], ALU.mult, ALU.add, [ST, HC, PK], [ST])
                _act(S, stb[:], st[:], AF.Copy, [ST], [STB])
            if dbg < 3:
                continue
            _mm(S, pS1[:], ones[:], ot[:], [CONST, OT], [PS1])
            _act(S, sq[:], ot[:], AF.Square, [OT], [SQ])
            _mm(S, pS2[:], ones[:], sq[:], [CONST, SQ], [PS2])
            _ts(S, mean[:], pS1[:], 1.0 / HD, ALU.mult, [PS1], [MEAN])
            _tt(S, sq[:], mean[:], mean[:], ALU.mult, [MEAN], [SQ])
            _stt(S, var[:], pS2[:], 1.0 / HD, sq[:], ALU.mult, ALU.subtract, [PS2, SQ], [VAR])
            S.op("dve", lambda E: E.tensor_scalar_add(var[:], var[:], 1e-5), rd=[VAR], wr=[VAR])
            _act(S, var[:], var[:], AF.Sqrt, [VAR], [VAR])
            S.op("dve", lambda E: E.reciprocal(out=var[:], in_=var[:]), rd=[VAR], wr=[VAR])
            _tt(S, ot[:], ot[:], mean[:], ALU.subtract, [OT, MEAN], [OT])
            _tt(S, ot[:], ot[:], var[:], ALU.mult, [OT, VAR], [OT])
            _stt(S, yb[:], ot[:], gnw[:, a:a + 1], gs[:], ALU.mult, ALU.mult, [OT, CONST, GS], [YB])
            S.op("sp", lambda E, a=a, T0=T0: E.dma_start(out=Y[a, :, T0:T0 + TT], in_=yb[:]), rd=[YB], wr=[YD], dma=YB)
    return YD


def emit_swa(S, nc, es, X1G, wmix, wbase, band_d, ident_d, Y, NQ, nah=NAH):
    SEQ = NQ * 2048
    NTI = SEQ // TT
    sb = lambda name, shape, dt: es.enter_context(nc.sbuf_tensor(name, shape, dt))
    pp = lambda name, shape, dt=F32: es.enter_context(nc.psum_tensor(name, shape, dt))
    R = S.res
    xb = sb("a_xb", [128, KC, TT], BF16); XB = R("a_xb")
    wbuf = sb("a_w", [128, 3, 4096], BF16); WB = [R(f"a_w{i}") for i in range(3)]
    onesb = sb("a_onesb", [128, 128], BF16); band = sb("a_band", [128, 256], F32); ident = sb("a_ident", [128, 128], BF16)
    CONST = R("a_const")
    S.op("dve", lambda E: E.memset(onesb[:], 1.0), wr=[CONST])
    S.op("sp", lambda E: E.dma_start(out=band[:], in_=band_d), wr=[CONST], dma=CONST)
    CONSTI = R("a_consti")
    S.op("pool", lambda E: E.dma_start(out=ident[:], in_=ident_d), wr=[CONSTI], dma=CONSTI)
    QT = sb("a_QT", [128, SEQ], BF16); KT = sb("a_KT", [128, SEQ], BF16); VTt = sb("a_VT", [128, SEQ], BF16)
    NUM = sb("a_NUM", [128, SEQ], F32); DEN = sb("a_DEN", [128, SEQ], F32)
    RQ, RK, RV, RN, RD = R("a_QT"), R("a_KT"), R("a_VT"), R("a_NUM"), R("a_DEN")
    vb = sb("a_vb", [128, 128], BF16); VB = R("a_vb")
    ef = sb("a_ef", [128, 256], F32); EF = R("a_ef")
    em = sb("a_em", [128, 256], BF16); EM = R("a_em")
    rec = sb("a_rec", [128, TT], F32); REC = R("a_rec")
    yb = sb("a_yb", [128, TT], F32); YB = R("a_yb")
    pA = [pp("a_pA0", [128, TT]), pp("a_pA1", [128, TT])]; PA = [R("a_pA0"), R("a_pA1")]
    pV = pp("a_pV", [128, TT])[:, 0:128]; PV = R("a_pV")
    pS = pp("a_pS", [128, TT])[:, 0:256]; PS = R("a_pS")
    pN = pp("a_pN", [128, TT])[:, 0:256]; PN = R("a_pN")
    pD = pp("a_pD", [128, TT])[:, 0:256]; PD = R("a_pD")
    YD = R("Ydram_a")
    pa = [0]
    for a in range(nah):
        for i in range(3):
            S.op("pool", lambda E, i=i, a=a: E.dma_start(out=wbuf[:, i, :], in_=wmix[wbase + 3 * a + i]), wr=[WB[i]], dma=WB[i])
        for ti in range(NTI):
            T0 = ti * TT
            src = x_tile_src(X1G, ti)
            S.op("pool", lambda E, src=src: E.dma_start(out=xb[:], in_=src), wr=[XB], dma=XB)
            for wi, (dst, DR, sc) in enumerate(((QT, RQ, HD ** -0.5), (KT, RK, 1.0), (VTt, RV, 1.0))):
                b = pa[0]; pa[0] ^= 1
                for kc in range(KC):
                    _mm(S, pA[b][:], wbuf[:, wi, kc * 128:(kc + 1) * 128], xb[:, kc, :], [WB[wi], XB], [PA[b]],
                        start=(kc == 0), stop=(kc == KC - 1))
                _act(S, dst[:, T0:T0 + TT], pA[b][:], AF.Copy, [PA[b]], [DR], scale=float(sc))
        S.op("dve", lambda E: E.memset(NUM[:], 0.0), wr=[RN])
        S.op("dve", lambda E: E.memset(DEN[:], 0.0), wr=[RD])
        for (window, r) in SWA_PAT:
            nb = SEQ // r // 128
            for rho in range(r):
                for m in range(nb):
                    k0 = rho + r * 128 * m
                    keys = slice(k0, k0 + r * 127 + 1, r)
                    nq = 256 if m + 1 < nb else 128
                    qs = slice(k0, k0 + r * (nq - 1) + 1, r)
                    _mm(S, pV, VTt[:, keys], ident[:], [RV, CONSTI], [PV])
                    _act(S, vb[:], pV, AF.Copy, [PV], [VB])
                    _mm(S, pS[:, 0:nq], KT[:, keys], QT[:, qs], [RK, RQ], [PS])
                    _act(S, ef[:, 0:nq], pS[:, 0:nq], AF.Exp, [PS], [EF])
                    _tt(S, em[:, 0:nq], ef[:, 0:nq], band[:, 0:nq], ALU.mult, [EF, CONST], [EM])
                    _mm(S, pN[:, 0:nq], vb[:], em[:, 0:nq], [VB, EM], [PN])
                    _mm(S, pD[:, 0:nq], onesb[:], em[:, 0:nq], [CONST, EM], [PD])
                    _tt(S, NUM[:, qs], NUM[:, qs], pN[:, 0:nq], ALU.add, [RN, PN], [RN])
                    _tt(S, DEN[:, qs], DEN[:, qs], pD[:, 0:nq], ALU.add, [RD, PD], [RD])
        for ti in range(NTI):
            T0 = ti * TT
            S.op("dve", lambda E, T0=T0: E.reciprocal(out=rec[:], in_=DEN[:, T0:T0 + TT]), rd=[RD], wr=[REC])
            _tt(S, yb[:], NUM[:, T0:T0 + TT], rec[:], ALU.mult, [RN, REC], [YB])
            S.op("sp", lambda E, a=a, T0=T0: E.dma_start(out=Y[NRH + a, :, T0:T0 + TT], in_=yb[:]), rd=[YB], wr=[YD], dma=YB)
    return YD

HD = 128
NDH = 8
DCW = 12 + 2 + 1


def emit_gdn(S, nc, es, X1G, wmix, wab_d, hconst, mats_d, ident_d, Y, NQ, nh=NDH, nti=None):
    SEQ = NQ * 2048
    NTI = nti or SEQ // TT
    sb = lambda name, shape, dt: es.enter_context(nc.sbuf_tensor(name, shape, dt))
    pp = lambda name: es.enter_context(nc.psum_tensor(name, [128, TT], F32))
    R = S.res
    xb = sb("d_xb", [128, KC, TT], BF16); XB = R("d_xb")
    wbuf = sb("d_w", [128, 4, 4096], BF16); WB = [R(f"d_w{i}") for i in range(4)]
    wab = sb("d_wab", [128, KC, 2 * NDH], BF16); WAB = R("d_wab")
    mats = sb("d_mats", [128, 6, 128], F32); MATS = R("d_mats")
    identb = sb("d_identb", [128, 128], BF16); IDB = R("d_identb")
    ones = sb("d_ones", [128, 128], F32); negones = sb("d_negones", [128, 128], F32); CONST = R("d_const")
    sel = sb("d_sel", [128, 2], F32)
    hc = sb("d_hc", [128, DCW], F32); HC = R("d_hc")
    S.op("dve", lambda E: E.memset(ones[:], 1.0), wr=[CONST])
    S.op("dve", lambda E: E.memset(negones[:], -1.0), wr=[CONST])
    S.op("dve", lambda E: E.memset(sel[:], 0.0), wr=[CONST])
    S.op("dve", lambda E: E.memset(sel[0:64, 0:1], 1.0), wr=[CONST])
    S.op("dve", lambda E: E.memset(sel[64:128, 1:2], 1.0), wr=[CONST])
    S.op("sp", lambda E: E.dma_start(out=mats[:], in_=mats_d.rearrange("k p c -> p k c")), wr=[MATS], dma=MATS)
    S.op("pool", lambda E: E.dma_start(out=identb[:], in_=ident_d), wr=[IDB], dma=IDB)
    S.op("pool", lambda E: E.dma_start(out=wab[:], in_=wab_d), wr=[WAB], dma=WAB)
    U, BONES, NEGLOW, NEGUP, STRICT, IDENT = (mats[:, k, :] for k in range(6))
    pA = [pp("d_pA0"), pp("d_pA1")]; PA = [R("d_pA0"), R("d_pA1")]
    pB = pp("d_pB"); PB = R("d_pB")
    pC = pp("d_pC"); PC = R("d_pC")
    pD = pp("d_pD"); PD = R("d_pD")
    pE = pp("d_pE"); PE_ = R("d_pE")
    pF = pp("d_pF"); PF = R("d_pF")
    pG = pp("d_pG"); PG = R("d_pG")
    pa = [0]
    raw = [sb(f"d_raw{i}", [128, 3 + TT], F32) for i in range(3)]; RAW = [R(f"d_raw{i}") for i in range(3)]
    acc = sb("d_acc", [128, TT], F32); ACC = R("d_acc")
    sq = sb("d_sq", [128, TT], F32); SQ = R("d_sq")
    rn = sb("d_rn", [128, TT], F32); RN = R("d_rn")
    qn = sb("d_qn", [128, TT], F32); QN = R("d_qn")
    qnb = sb("d_qnb", [128, TT], BF16); QNB = R("d_qnb")
    knb = sb("d_knb", [128, TT], BF16); KNB = R("d_knb")
    vcb = sb("d_vcb", [128, TT], BF16); VCB = R("d_vcb")
    gs = sb("d_gs", [128, TT], F32); GS = R("d_gs")
    ot = sb("d_ot", [128, TT], F32); OT = R("d_ot")
    yb = sb("d_yb", [128, TT], F32); YB = R("d_yb")
    ab = sb("d_ab", [128, 2], F32); AB = R("d_ab")
    beta = sb("d_beta", [128, 1], F32); BETA = R("d_beta")
    nbeta = sb("d_nbeta", [128, 1], F32); NBETA = R("d_nbeta")
    gcol = sb("d_g", [128, 1], F32); GC = R("d_g")
    nega = sb("d_nega", [128, 1], F32); NEGA = R("d_nega")
    ug = sb("d_ug", [128, 128], F32); UG = R("d_ug")
    gsel = sb("d_gsel", [128, 2], F32); GSEL = R("d_gsel")
    cols = sb("d_cols", [128, 4], F32); COLS = R("d_cols")
    egl = sb("d_egl", [128, 2], F32); EGL = R("d_egl")
    dm = sb("d_dm", [128, 128], F32); DM = R("d_dm")
    el = sb("d_el", [128, 128], F32); EL = R("d_el")
    et = sb("d_et", [128, 128], F32); ET = R("d_et")
    eg = sb("d_eg", [128, 128], F32); EG = R("d_eg")
    ktm = sb("d_ktm", [128, 128], F32); KTM = R("d_ktm")
    kbg = sb("d_kbg", [128, 128], BF16); KBG = R("d_kbg")
    kd = sb("d_kd", [128, 128], BF16); KD = R("d_kd")
    vbt = sb("d_vbt", [128, 128], BF16); VBT = R("d_vbt")
    X = [sb(f"d_X{i}", [128, 128], F32) for i in range(2)]; XR = [R(f"d_X{i}") for i in range(2)]
    XT = [sb(f"d_XT{i}", [128, 128], F32) for i in range(2)]; XTR = [R(f"d_XT{i}") for i in range(2)]
    Yk = sb("d_Yk", [128, 128], F32); YK = R("d_Yk")
    Q = [sb(f"d_Q{i}", [128, 128], F32) for i in range(2)]; QRs = [R(f"d_Q{i}") for i in range(2)]
    ttb = sb("d_ttb", [128, 128], BF16); TTB = R("d_ttb")
    ub = sb("d_u", [128, 128], F32); UB = R("d_u")
    wtb = sb("d_wt", [128, 128], BF16); WTB = R("d_wt")
    qgb = sb("d_qg", [128, 128], BF16); QGB = R("d_qg")
    qkt = sb("d_qkt", [128, 128], BF16); QKT = R("d_qkt")
    vnew = sb("d_vnew", [128, 128], BF16); VNEW = R("d_vnew")
    st = sb("d_st", [128, 128], F32); ST = R("d_st")
    stb = sb("d_stb", [128, 128], BF16); STB = R("d_stb")
    YD = R("Ydram_d")

    def proj_fm(wi):
        b = pa[0]; pa[0] ^= 1
        for kc in range(KC):
            _mm(S, pA[b][:], wbuf[:, wi, kc * 128:(kc + 1) * 128], xb[:, kc, :], [WB[wi], XB], [PA[b]],
                start=(kc == 0), stop=(kc == KC - 1))
        return pA[b], PA[b]

    def conv_silu(i, pin, PIN, cw0):
        r_, R_ = raw[i], RAW[i]
        _act(S, r_[:, 3:3 + TT], pin[:], AF.Copy, [PIN], [R_])
        _ts(S, acc[:], r_[:, 0:TT], hc[:, cw0:cw0 + 1], ALU.mult, [R_, HC], [ACC])
        for k in range(1, 4):
            _stt(S, acc[:], r_[:, k:k + TT], hc[:, cw0 + k:cw0 + k + 1], acc[:], ALU.mult, ALU.add, [R_, HC, ACC], [ACC])
        _act(S, r_[:, 0:3], r_[:, TT:TT + 3], AF.Copy, [R_], [R_])
        _act(S, acc[:], acc[:], AF.Silu, [ACC], [ACC])

    def l2n(dst, DST, scale):
        _act(S, sq[:], acc[:], AF.Square, [ACC], [SQ])
        _mm(S, pB[:], ones[:], sq[:], [CONST, SQ], [PB])
        _ts(S, rn[:], pB[:], 1e-6, ALU.add, [PB], [RN])
        _act(S, rn[:], rn[:], AF.Sqrt, [RN], [RN])
        S.op("dve", lambda E: E.reciprocal(out=rn[:], in_=rn[:]), rd=[RN], wr=[RN])
        _stt(S, dst[:], acc[:], float(scale), rn[:], ALU.mult, ALU.mult, [ACC, RN], [DST])

    for a in range(nh):
        for i in range(4):
            S.op("pool", lambda E, i=i, a=a: E.dma_start(out=wbuf[:, i, :], in_=wmix[4 * a + i]), wr=[WB[i]], dma=WB[i])
        S.op("sp", lambda E, a=a: E.dma_start(out=hc[:], in_=hconst[a]), wr=[HC], dma=HC)
        S.op("dve", lambda E: E.memset(st[:], 0.0), wr=[ST])
        S.op("dve", lambda E: E.memset(stb[:], 0.0), wr=[STB])
        for i in range(3):
            S.op("dve", lambda E, i=i: E.memset(raw[i][:, 0:3], 0.0), wr=[RAW[i]])
        _act(S, nega[:], hc[:, 12:13], AF.Exp, [HC], [NEGA])
        _ts(S, nega[:], nega[:], -1.0, ALU.mult, [NEGA], [NEGA])
        for ti in range(NTI):
            T0 = ti * TT
            src = x_tile_src(X1G, ti)
            if getattr(S, "pf", None) is not None:
                S.pf.step(2)
            S.op("pool", lambda E, src=src: E.dma_start(out=xb[:], in_=src), wr=[XB], dma=XB)
            p, P = proj_fm(0); conv_silu(0, p, P, 0); l2n(qn, QN, HD ** -0.5)
            _act(S, qnb[:], qn[:], AF.Copy, [QN], [QNB])
            p, P = proj_fm(1); conv_silu(1, p, P, 4); l2n(knb, KNB, 1.0)
            p, P = proj_fm(2); conv_silu(2, p, P, 8)
            _act(S, vcb[:], acc[:], AF.Copy, [ACC], [VCB])
            p, P = proj_fm(3)
            _act(S, gs[:], p[:], AF.Silu, [P], [GS])
            for c in range(4):
                cs_ = slice(c * 128, (c + 1) * 128)
                for kc in range(KC):
                    _mm(S, pC[:, 0:1], xb[:, kc, cs_], wab[:, kc, a:a + 1], [XB, WAB], [PC], start=(kc == 0), stop=(kc == KC - 1))
                for kc in range(KC):
                    _mm(S, pC[:, 1:2], xb[:, kc, cs_], wab[:, kc, NDH + a:NDH + a + 1], [XB, WAB], [PC], start=(kc == 0), stop=(kc == KC - 1))
                _act(S, ab[:], pC[:, 0:2], AF.Copy, [PC], [AB])
                _act(S, beta[:], ab[:, 1:2], AF.Sigmoid, [AB], [BETA])
                _ts(S, nbeta[:], beta[:], -1.0, ALU.mult, [BETA], [NBETA])
                _act(S, gcol[:], ab[:, 0:1], AF.Exp, [AB, HC], [GC], bias=hc[:, 13:14])
                _ts(S, gcol[:], gcol[:], 1.0, ALU.add, [GC], [GC])
                _act(S, gcol[:], gcol[:], AF.Ln, [GC], [GC])
                _tt(S, gcol[:], gcol[:], nega[:], ALU.mult, [GC, NEGA], [GC])
                _ts(S, ug[:], U, gcol[:, 0:1], ALU.mult, [MATS, GC], [UG])
                _ts(S, gsel[:], sel[:], gcol[:, 0:1], ALU.mult, [CONST, GC], [GSEL])
                _mm(S, pC[:, 8:9], U, gcol[:], [MATS, GC], [PC])
                _mm(S, pC[:, 9:10], BONES, gcol[:], [MATS, GC], [PC])
                _mm(S, pC[:, 16:18], ones[:], gsel[:], [CONST, GSEL], [PC])
                _act(S, cols[:, 0:2], pC[:, 8:10], AF.Copy, [PC], [COLS])
                _act(S, egl[:], pC[:, 16:18], AF.Exp, [PC], [EGL])
                _act(S, cols[:, 2:3], cols[:, 0:1], AF.Exp, [COLS], [COLS])
                _tt(S, cols[:, 3:4], cols[:, 1:2], cols[:, 0:1], ALU.subtract, [COLS], [COLS])
                _act(S, cols[:, 3:4], cols[:, 3:4], AF.Exp, [COLS], [COLS])
                _mm(S, pD[:, 0:128], ug[:], ones[:], [UG, CONST], [PD], start=True, stop=False)
                _mm(S, pD[:, 0:128], negones[:], ug[:], [UG, CONST], [PD], start=False, stop=True)
                _tt(S, dm[:], pD[:, 0:128], NEGLOW, ALU.add, [PD, MATS], [DM])
                _act(S, el[:], dm[:], AF.Exp, [DM], [EL])
                _tt(S, el[:], el[:], STRICT, ALU.mult, [EL, MATS], [EL])
                _mm(S, pD[:, 128:256], ones[:], ug[:], [UG, CONST], [PD], start=True, stop=False)
                _mm(S, pD[:, 128:256], ug[:], negones[:], [UG, CONST], [PD], start=False, stop=True)
                _tt(S, dm[:], pD[:, 128:256], NEGUP, ALU.add, [PD, MATS], [DM])
                _act(S, et[:], dm[:], AF.Exp, [DM], [ET])
                _mm(S, pD[:, 256:384], ones[:], ug[:], [UG, CONST], [PD])
                _act(S, eg[:], pD[:, 256:384], AF.Exp, [PD], [EG])
                _mm(S, pC[:, 128:256], knb[:, cs_], identb[:], [KNB, IDB], [PC])
                _act(S, ktm[:], pC[:, 128:256], AF.Copy, [PC], [KTM])
                _ts(S, kbg[:], ktm[:], beta[:, 0:1], ALU.mult, [KTM, BETA, COLS], [KBG], s2=cols[:, 2:3], op1=ALU.mult)
                _ts(S, kd[:], ktm[:], cols[:, 3:4], ALU.mult, [KTM, COLS], [KD])
                _mm(S, pC[:, 256:384], vcb[:, cs_], identb[:], [VCB, IDB], [PC])
                _ts(S, vbt[:], pC[:, 256:384], beta[:, 0:1], ALU.mult, [PC, BETA], [VBT])
                _mm(S, pE[:, 0:128], knb[:, cs_], knb[:, cs_], [KNB], [PE_])
                _stt(S, X[0][:], pE[:, 0:128], nbeta[:, 0:1], el[:], ALU.mult, ALU.mult, [PE_, NBETA, EL], [XR[0]])
                _mm(S, pE[:, 128:256], X[0][:], IDENT, [XR[0], MATS], [PE_])
                _act(S, XT[0][:], pE[:, 128:256], AF.Copy, [PE_], [XTR[0]])
                _tt(S, Q[0][:], XT[0][:], IDENT, ALU.add, [XTR[0], MATS], [QRs[0]])
                cur = 0
                for lvl in range(5):
                    nxt = cur ^ 1
                    _mm(S, pE[:, 0:128], XT[cur][:], X[cur][:], [XTR[cur], XR[cur]], [PE_])
                    _mm(S, pE[:, 128:256], X[cur][:], XT[cur][:], [XTR[cur], XR[cur]], [PE_])
                    _act(S, X[nxt][:], pE[:, 0:128], AF.Copy, [PE_], [XR[nxt]])
                    _act(S, XT[nxt][:], pE[:, 128:256], AF.Copy, [PE_], [XTR[nxt]])
                    _tt(S, Yk[:], X[nxt][:], IDENT, ALU.add, [XR[nxt], MATS], [YK])
                    _mm(S, pE[:, 256:384], Yk[:], Q[cur][:], [YK, QRs[cur]], [PE_])
                    _act(S, Q[nxt][:], pE[:, 256:384], AF.Copy, [PE_], [QRs[nxt]])
                    cur = nxt
                _act(S, ttb[:], Q[cur][:], AF.Copy, [QRs[cur]], [TTB])
                _mm(S, pF[:, 0:128], ttb[:], vbt[:], [TTB, VBT], [PF])
                _act(S, ub[:], pF[:, 0:128], AF.Copy, [PF], [UB])
                _mm(S, pF[:, 128:256], kbg[:], ttb[:], [KBG, TTB], [PF])
                _act(S, wtb[:], pF[:, 128:256], AF.Copy, [PF], [WTB])
                _tt(S, qgb[:], qn[:, cs_], eg[:], ALU.mult, [QN, EG], [QGB])
                _mm(S, pF[:, 256:384], knb[:, cs_], qnb[:, cs_], [KNB, QNB], [PF])
                _tt(S, qkt[:], pF[:, 256:384], et[:], ALU.mult, [PF, ET], [QKT])
                for h2 in range(2):
                    ps_ = slice(h2 * 64, (h2 + 1) * 64)
                    _mm(S, pF[:, 0:128], wtb[:], stb[:], [WTB, STB], [PF])
                    _tt(S, vnew[ps_, :], ub[ps_, :], pF[ps_, 0:128], ALU.subtract, [UB, PF], [VNEW])
                    _mm(S, pG[:, c * 128 + h2 * 64:c * 128 + (h2 + 1) * 64], stb[:], qgb[:, ps_], [STB, QGB], [PG], start=True, stop=False)
                    _mm(S, pG[:, c * 128 + h2 * 64:c * 128 + (h2 + 1) * 64], vnew[ps_, :], qkt[ps_, ps_], [VNEW, QKT], [PG], start=False, stop=True)
                    _mm(S, pF[:, 128:256], kd[ps_, :], vnew[ps_, :], [KD, VNEW], [PF])
                    _stt(S, st[:], st[:], egl[:, h2:h2 + 1], pF[:, 128:256], ALU.mult, ALU.add, [ST, EGL, PF], [ST])
                    _act(S, stb[:], st[:], AF.Copy, [ST], [STB])
            _act(S, ot[:], pG[:], AF.Copy, [PG], [OT])
            _act(S, sq[:], ot[:], AF.Square, [OT], [SQ])
            _mm(S, pB[:], ones[:], sq[:], [CONST, SQ], [PB])
            _ts(S, rn[:], pB[:], 1.0 / HD, ALU.mult, [PB], [RN], s2=1e-6, op1=ALU.add)
            _act(S, rn[:], rn[:], AF.Sqrt, [RN], [RN])
            S.op("dve", lambda E: E.reciprocal(out=rn[:], in_=rn[:]), rd=[RN], wr=[RN])
            _tt(S, ot[:], ot[:], rn[:], ALU.mult, [OT, RN], [OT])
            _stt(S, yb[:], ot[:], hc[:, 14:15], gs[:], ALU.mult, ALU.mult, [OT, HC, GS], [YB])
            S.op("sp", lambda E, a=a, T0=T0: E.dma_start(out=Y[a, :, T0:T0 + TT], in_=yb[:]), rd=[YB], wr=[YD], dma=YB)
    return YD


def emit_allgather_x(S, nc, X, XG, ccres):
    XR = S.res("agx_src"); XGR = S.res("agx_dst")
    for i in range(KC * 2):
        src = X.ap()[i * 128:(i + 1) * 128, :]
        dst = XG.ap()[i * 512:(i + 1) * 512, :]
        S.op("pool", lambda E, src=src, dst=dst: E.collective_compute(
            "AllGather", ALU.bypass, replica_groups=[[0, 1, 2, 3], [4, 5, 6, 7]], ins=[src.opt()], outs=[dst.opt()]),
            rd=[XR], wr=[XGR], dma=ccres, cc=True)


def emit_reducescatter(S, nc, P, M, ccres):
    PR = S.res("rs_src"); MR = S.res("rs_dst")
    for dc in range(KC):
        src = P.ap()[dc * 512:(dc + 1) * 512, :]
        dst = M.ap()[dc * 128:(dc + 1) * 128, :]
        S.op("pool", lambda E, src=src, dst=dst: E.collective_compute(
            "ReduceScatter", ALU.add, replica_groups=[[0, 1, 2, 3], [4, 5, 6, 7]], ins=[src.opt()], outs=[dst.opt()]),
            rd=[PR], wr=[MR], dma=ccres, cc=True)


def emit_outproj(S, nc, es, Y, wo_d, P, NQ=4, sfx=""):
    SEQ = NQ * 2048
    sb = lambda name, shape, dt: es.enter_context(nc.sbuf_tensor(name + sfx, shape, dt))
    R = S.res
    wo = sb("o_wo", [128, 8, 4096], BF16); WO = R("o_wo")
    yb = [sb(f"o_yb{i}", [128, 8, TT], BF16) for i in range(2)]; YB = [R(f"o_yb{i}") for i in range(2)]
    ob = [sb(f"o_ob{i}", [128, TT], F32) for i in range(2)]; OB = [R(f"o_ob{i}") for i in range(2)]
    ps = [es.enter_context(nc.psum_tensor(f"o_ps{i}" + sfx, [128, TT], F32)) for i in range(2)]; PS = [R(f"o_ps{i}") for i in range(2)]
    PD = R("o_P")
    S.op("pool", lambda E: E.dma_start(out=wo[:], in_=wo_d.rearrange("k p c -> p k c")), wr=[WO], dma=WO)
    n = 0
    for ti in range(SEQ // TT):
        T0 = ti * TT
        q, o = T0 // 2048, T0 % 2048
        b = ti % 2
        S.op("pool", lambda E, b=b, T0=T0: E.dma_start(out=yb[b][:], in_=Y[:, :, T0:T0 + TT].rearrange("k p t -> p k t")),
             wr=[YB[b]], dma=YB[b])
        for dc in range(KC):
            k = n % 2; n += 1
            for kc in range(8):
                _mm(S, ps[k][:], wo[:, kc, dc * 128:(dc + 1) * 128], yb[b][:, kc, :], [WO, YB[b]], [PS[k]],
                    start=(kc == 0), stop=(kc == 7))
            _act(S, ob[k][:], ps[k][:], AF.Copy, [PS[k]], [OB[k]])
            S.op("sp", lambda E, k=k, dc=dc, q=q, o=o: E.dma_start(out=P[dc, q, :, o:o + TT], in_=ob[k][:]),
                 rd=[OB[k]], wr=[PD], dma=OB[k])


def emit_resid_ln(S, nc, es, Xin, M, Xout, lnw, lnb, NT, sfx=""):
    sb = lambda name, shape, dt: es.enter_context(nc.sbuf_tensor(name + sfx, shape, dt))
    R = S.res
    A = {}
    A["ones"] = sb("l_ones", [128, 128], F32)
    A["lnw"] = sb("l_lnw", [128, KC], F32); A["lnb"] = sb("l_lnb", [128, KC], F32)
    rbuf = sb("l_r", [128, KC * TT], F32)
    r = rbuf[:].rearrange("p (c t) -> p c t", c=KC)
    Rr = [R(f"l_R{d}") for d in range(KC)]
    ltb = sb("l_lt", [128, 10, TT], F32); lt = [ltb[:, k, :] for k in range(10)]; LT = [R(f"l_LT{k}") for k in range(10)]
    x32 = sb("l_x32", [128, 2, TT], F32); X32 = [R("l_x32a"), R("l_x32b")]
    m32 = sb("l_m32", [128, 2, TT], F32); M32 = [R("l_m32a"), R("l_m32b")]
    ps = [None] * 6 + [es.enter_context(nc.psum_tensor("l_ps6" + sfx, [128, TT], F32)), es.enter_context(nc.psum_tensor("l_ps7" + sfx, [128, TT], F32))]
    psr = [None] * 6 + [R("l_ps6"), R("l_ps7")]
    CONST = R("l_const"); LNP = R("l_lnp")
    S.op("dve", lambda E: E.memset(A["ones"][:], 1.0), wr=[CONST])
    S.op("sp", lambda E: E.dma_start(out=A["lnw"][:], in_=lnw), wr=[LNP], dma=LNP)
    S.op("sp", lambda E: E.dma_start(out=A["lnb"][:], in_=lnb), wr=[LNP], dma=LNP)
    for ti in range(NT):
        t0 = ti * TT
        for dc in range(KC):
            b = dc % 2
            xs_ = rl_one(Xin, dc, t0)
            ms_ = M[dc, :, t0:t0 + TT]
            S.op("sp", lambda E, b=b, xs_=xs_: E.dma_start(out=x32[:, b, :], in_=xs_), wr=[X32[b]], dma=X32[b])
            S.op("sp", lambda E, b=b, ms_=ms_: E.dma_start(out=m32[:, b, :], in_=ms_), wr=[M32[b]], dma=M32[b])
            _stt(S, r[:, dc, :], x32[:, b, :], float(ALPHA), m32[:, b, :], ALU.mult, ALU.add, [X32[b], M32[b]], [Rr[dc]])
        emit_ln(S, nc, A, r, Rr, lt, LT, ps, psr, LNP, CONST, Xout, t0)

D_MODEL = 4096
D_FF = 11008
NFC = D_FF // 128
NCORE = 8
TOK_PER_CORE = 2048
NT = TOK_PER_CORE // TT
SEQ = 8192
NSCW = NFC // 2

_PROG = None


def _build(debug=False):
    nc = bass.Bass("TRN2", target_bir_lowering=False)

    def ext(name, shape):
        return nc.dram_tensor(name, shape, F32, kind="ExternalInput").ap()

    RL = [KC * 2 * 128, 1024]
    rlv = lambda t: t.ap().rearrange("(c h p) t -> c h p t", c=KC, h=2)
    x_in = nc.dram_tensor("x", RL, F32, kind="ExternalInput")
    y_out = nc.dram_tensor("y", RL, F32, kind="ExternalOutput")
    fw_ = [(ext(f"wg{k}", [NSCW * 64, 4096]), ext(f"wu{k}", [NSCW * 64, 4096]), ext(f"wd{k}", [2 * KC * 16, NFC * 128]))
           for k in range(4)]
    lnw = [ext(f"lnw{i}", [128, KC]) for i in range(6)]
    lnb = [ext(f"lnb{i}", [128, KC]) for i in range(6)]
    wmix_ab = ext("wmix_ab", [28, 128, 4096]); cos2 = ext("cos2", [128, SEQ]); sin2 = ext("sin2", [128, SEQ])
    hconst_ab = ext("hconst_ab", [4, 128, HCW]); perm = ext("perm", [128, 128]); ident = ext("ident", [128, 128])
    band = ext("band", [128, 256]); gnw = ext("gnw", [128, 4]); wo_ab = ext("wo_ab", [8, 128, 4096])
    wmix_dn = ext("wmix_dn", [32, 128, 4096]); wab = ext("wab", [128, KC, 16]); hconst_dn = ext("hconst_dn", [8, 128, DCW])
    mats = ext("mats", [6, 128, 128]); wo_dn = ext("wo_dn", [8, 128, 4096])
    Xs = [nc.dram_tensor(f"X{i}", RL, F32) for i in range(5)]
    XG = nc.dram_tensor("XG", [KC * 2 * 4 * 128, 1024], F32)
    Yd = nc.dram_tensor("Yd", [8, 128, SEQ], F32)
    Pd = nc.dram_tensor("Pd", [KC * 4 * 128, 2048], F32)
    Md = nc.dram_tensor("Md", [KC * 128, 2048], F32)
    gbs = [{"wg": (nc.dram_tensor(f"wg_b{j}", [NSCW * 64, 4096], F32), nc.dram_tensor(f"wg_g{j}", [NSCW * 256, 4096], F32)),
            "wu": (nc.dram_tensor(f"wu_b{j}", [NSCW * 64, 4096], F32), nc.dram_tensor(f"wu_g{j}", [NSCW * 256, 4096], F32)),
            "wd": (nc.dram_tensor(f"wd_b{j}", [2 * KC * 16, NFC * 128], F32), nc.dram_tensor(f"wd_g{j}", [2 * KC * 64, NFC * 128], F32))}
           for j in range(2)]
    XGv = XG.ap().rearrange("(c h q p) t -> c h q p t", c=KC, h=2, q=4)
    Yv = Yd.ap()
    Pv = Pd.ap().rearrange("(c q p) t -> c q p t", c=KC, q=4)
    Mv = Md.ap().rearrange("(c p) t -> c p t", c=KC)

    with ExitStack() as es0:
        S = Sched(nc, es0)
        S.out_res = S.res("out"); S.out_res.keep = True
        CC = S.res("cc"); CC.keep = True

        S.pf = None
        gath = {}

        def make_gathers(k):
            gb = gbs[k % 2]
            gath[k] = (Gather(S, nc, f"wg{k}", fw_[k][0], NSCW, 64, 4096, CC, bufs=gb["wg"]),
                       Gather(S, nc, f"wu{k}", fw_[k][1], NSCW, 64, 4096, CC, bufs=gb["wu"]),
                       Gather(S, nc, f"wd{k}", fw_[k][2], 2 * KC, 16, NFC * 128, CC, bufs=gb["wd"]))

        def start_prefetch(k):
            make_gathers(k)
            S.pf = Prefetcher(S, gath[k])

        def end_prefetch():
            if S.pf is not None:
                S.pf.finish()
                S.pf = None

        def ffn_stage(k, Xin, Xout, li, prefetch_next=None):
            with ExitStack() as es:
                A = ffn_alloc(S, nc, es, NFC, sfx=f"_{k}")
                ws = WStream(S, nc, es, 5, name=f"wr{k}")
                if k not in gath:
                    make_gathers(k)
                if prefetch_next is not None:
                    start_prefetch(prefetch_next)
                g1, g2, g3 = gath[k]
                emit_ffn_ln(S, nc, A, ws, Xin, Xout, g1, g2, g3, lnw[li], lnb[li], NT, NFC)
                end_prefetch()
                S.barrier_all(); S.flush()

        def exch_ag(X):
            emit_allgather_x(S, nc, X, XG, CC)
            S.barrier_all(); S.flush()

        def outproj_rs(wo_d, sfx):
            with ExitStack() as es:
                emit_outproj(S, nc, es, Yv, wo_d, Pv, sfx=sfx)
                S.barrier_all(); S.flush()
            emit_reducescatter(S, nc, Pd, Md, CC)
            S.barrier_all(); S.flush()

        def resid_ln(Xin, Xout, li):
            with ExitStack() as es:
                emit_resid_ln(S, nc, es, Xin, Mv, Xout, lnw[li], lnb[li], NT, sfx=f'_{li}')
                S.barrier_all(); S.flush()

        X = [rlv(t) for t in Xs]
        DBG = S.res("dbg"); DBG.keep = True

        def dump(name, src_ap, shape):
            if not debug:
                return
            o = nc.dram_tensor(name, shape, F32, kind="ExternalOutput").ap()
            S.op("sp", lambda E: E.dma_start(out=o, in_=src_ap), wr=[DBG], dma=DBG)
            S.barrier_all(); S.flush()
        ffn_stage(0, rlv(x_in), X[0], 0)
        exch_ag(Xs[0])
        dump('dump_XG', XG.ap()[0:1024, :], [1024, 1024])
        with ExitStack() as es:
            start_prefetch(1)
            emit_ret(S, nc, es, XGv, wmix_ab, cos2, sin2, hconst_ab, perm, ident, gnw, Yv, 4)
            end_prefetch()
            S.barrier_all(); S.flush()
        with ExitStack() as es:
            emit_swa(S, nc, es, XGv, wmix_ab, 16, band, ident, Yv, 4)
            S.barrier_all(); S.flush()
        dump('dump_Y', Yd.ap(), [8, 128, SEQ])
        outproj_rs(wo_ab, '_a')
        dump('dump_P', Pd.ap()[0:512, :], [512, 2048])
        dump('dump_M', Md.ap(), [KC * 128, 2048])
        resid_ln(X[0], X[1], 1)
        ffn_stage(1, X[1], X[2], 2, prefetch_next=2)
        ffn_stage(2, X[2], X[3], 3)
        exch_ag(Xs[3])
        with ExitStack() as es:
            start_prefetch(3)
            emit_gdn(S, nc, es, XGv, wmix_dn, wab, hconst_dn, mats, ident, Yv, 4)
            end_prefetch()
            S.barrier_all(); S.flush()
        outproj_rs(wo_dn, '_d')
        resid_ln(X[3], X[4], 4)
        ffn_stage(3, X[4], rlv(y_out), 5)
        for i in range(5):
            dump(f'dump_X{i}', Xs[i].ap(), RL)
    return nc


def _prep_w_in(W):
    Dm, F = W.shape
    return W.reshape(Dm // 128, 128, F // 128, 128).transpose(2, 1, 0, 3).reshape(F // 128, 128, (Dm // 128) * 128)


def _prep_w_out(W):
    F, Dm = W.shape
    return W.reshape(F // 128, 128, Dm // 128, 128).transpose(2, 1, 0, 3).reshape(Dm // 128, 128, (F // 128) * 128)


def _shard_in(wp, r):
    n = wp.shape[0]
    return np.ascontiguousarray(wp.reshape(n // 2, 4, 64, 4096)[:, r].reshape((n // 2) * 64, 4096))


def _shard_out(wp, r):
    return np.ascontiguousarray(wp.reshape(2 * KC, 4, 16, -1)[:, r].reshape(2 * KC * 16, -1))


def _prep_chunk(Wc):
    return np.ascontiguousarray(Wc.reshape(32, 128, 128).transpose(1, 0, 2).reshape(128, 4096))


def _vec(v):
    return np.ascontiguousarray(np.asarray(v, np.float32).reshape(-1, 128).T)


_PERMIDX = np.concatenate([np.arange(0, 128, 2), np.arange(1, 128, 2)])


def _const_tables():
    m = {}
    pos = np.arange(SEQ, dtype=np.float32)
    inv_freq = (1.0 / (np.float32(10000.0) ** np.linspace(0.0, 1.0, 64, dtype=np.float32))).astype(np.float32)
    ang = (pos[None, :] * inv_freq[:, None]).astype(np.float32)
    m["cos2"] = np.concatenate([np.cos(ang), np.cos(ang)], 0).astype(np.float32)
    m["sin2"] = np.concatenate([-np.sin(ang), np.sin(ang)], 0).astype(np.float32)
    p = np.zeros((128, 128), np.float32); p[(np.arange(128) + 64) % 128, np.arange(128)] = 1
    m["perm"] = p
    m["ident"] = np.eye(128, dtype=np.float32)
    d = np.arange(256)[None, :] - np.arange(128)[:, None]
    m["band"] = ((d >= 0) & (d <= 128)).astype(np.float32)
    idx = np.arange(128); same = (idx[:, None] // 64) == (idx[None, :] // 64)
    U = (same & (idx[:, None] <= idx[None, :])).astype(np.float32)
    low = same & (idx[:, None] >= idx[None, :])
    NEGLOW = np.where(low, 0.0, -1e5).astype(np.float32)
    STRICT = (same & (idx[:, None] > idx[None, :])).astype(np.float32)
    m["mats"] = np.stack([U, same.astype(np.float32), NEGLOW, np.ascontiguousarray(NEGLOW.T), STRICT, np.eye(128, dtype=np.float32)])
    return m


def _ret_head_consts(g):
    hc = np.zeros((4, 128, HCW), np.float32)
    idx = np.arange(128, dtype=np.float32)
    for a in range(4):
        h = 4 * g + a
        lg = np.log1p(-np.exp2(np.float32(-5.0 - h))).astype(np.float32)
        hc[a, :, 0:512] = np.tile(np.exp(lg * (idx + 1)), 4)[None, :]
        rel = idx[None, :] - idx[:, None]
        hc[a, :, 512:640] = np.where(rel >= 0, np.exp(lg * np.maximum(rel, 0)), 0)
        hc[a, :, 640] = np.exp(lg * (127 - idx))
        hc[a, :, 641] = np.exp(lg * 128)
    return hc


def kernel(x, ffn_w_gate, ffn_w_up, ffn_w_down, ln_w, ln_b, ab_w_in, ab_gn_w, ab_w_out,
           dn_w_in, dn_conv_w, dn_a_log, dn_dt_bias, dn_norm_w, dn_w_out):
    global _PROG
    from concourse.bass_utils import run_bass_kernel_spmd
    if _PROG is None:
        _PROG = _build()
    in_maps = _prepare(x, ffn_w_gate, ffn_w_up, ffn_w_down, ln_w, ln_b, ab_w_in, ab_gn_w, ab_w_out,
                       dn_w_in, dn_conv_w, dn_a_log, dn_dt_bias, dn_norm_w, dn_w_out)
    res = run_bass_kernel_spmd(_PROG, in_maps, core_ids=list(range(NCORE)))
    return _assemble(res.results, np.asarray(x).shape)


def _assemble(results, shape):
    B, Sq, Dm = shape
    outs = []
    for r in results:
        yc = np.asarray(r["y"], np.float32).reshape(KC, 2, 128, 1024).transpose(1, 3, 0, 2).reshape(TOK_PER_CORE, Dm)
        outs.append(yc)
    return np.ascontiguousarray(np.concatenate(outs, axis=0).reshape(B, Sq, Dm).astype(np.float32))


def _prepare(x, ffn_w_gate, ffn_w_up, ffn_w_down, ln_w, ln_b, ab_w_in, ab_gn_w, ab_w_out,
             dn_w_in, dn_conv_w, dn_a_log, dn_dt_bias, dn_norm_w, dn_w_out):
    f32 = np.float32
    x = np.asarray(x, f32)
    B, Sq, Dm = x.shape
    xf = x.reshape(B * Sq, Dm)
    common = _const_tables()
    for i, (l, s) in enumerate(((0, 0), (0, 1), (0, 2), (1, 0), (1, 1), (1, 2))):
        common[f"lnw{i}"] = _vec(ln_w[l, s]); common[f"lnb{i}"] = _vec(ln_b[l, s])
    ffn_sh = {r: {} for r in range(4)}
    for k, (l, s) in enumerate(((0, 0), (0, 1), (1, 0), (1, 1))):
        wgp = _prep_w_in(np.asarray(ffn_w_gate[l, s], f32)); wup = _prep_w_in(np.asarray(ffn_w_up[l, s], f32))
        wdp = _prep_w_out(np.asarray(ffn_w_down[l, s], f32))
        for r in range(4):
            ffn_sh[r][f"wg{k}"] = _shard_in(wgp, r); ffn_sh[r][f"wu{k}"] = _shard_in(wup, r); ffn_sh[r][f"wd{k}"] = _shard_out(wdp, r)
        del wgp, wup, wdp
    abw = np.asarray(ab_w_in[0], f32); abo = np.asarray(ab_w_out[0], f32); gn = np.asarray(ab_gn_w[0], f32)
    dnw = np.asarray(dn_w_in[0], f32); dno = np.asarray(dn_w_out[0], f32); cw = np.asarray(dn_conv_w[0], f32)
    alog = np.asarray(dn_a_log[0], f32); dtb = np.asarray(dn_dt_bias[0], f32); nw = np.asarray(dn_norm_w[0], f32)
    grp = {}
    for g in range(4):
        m = {}
        ch = []
        for a in range(4):
            h = 4 * g + a
            ch.append(_prep_chunk(abw[:, h * 128:(h + 1) * 128][:, _PERMIDX]))
            ch.append(_prep_chunk(abw[:, 2048 + h * 128:2048 + (h + 1) * 128][:, _PERMIDX]))
            ch.append(_prep_chunk(abw[:, 4096 + h * 128:4096 + (h + 1) * 128]))
            ch.append(_prep_chunk(abw[:, 6144 + h * 128:6144 + (h + 1) * 128]))
        for a in range(4):
            h = 4 * g + a
            for o in (8192, 10240, 12288):
                ch.append(_prep_chunk(abw[:, o + h * 128:o + (h + 1) * 128]))
        m["wmix_ab"] = np.stack(ch)
        m["hconst_ab"] = _ret_head_consts(g)
        m["gnw"] = np.ascontiguousarray(gn.reshape(16, 128)[4 * g:4 * g + 4].T)
        rows = [abo[(4 * g + a) * 128:(4 * g + a + 1) * 128] for a in range(4)] + \
               [abo[2048 + (4 * g + a) * 128:2048 + (4 * g + a + 1) * 128] for a in range(4)]
        m["wo_ab"] = np.ascontiguousarray(np.stack(rows))
        ch = []
        for a in range(8):
            h = 8 * g + a
            for o in (0, 4096, 8192, 12288):
                ch.append(_prep_chunk(dnw[:, o + h * 128:o + (h + 1) * 128]))
        m["wmix_dn"] = np.stack(ch)
        colsel = [16384 + 8 * g + a for a in range(8)] + [16416 + 8 * g + a for a in range(8)]
        m["wab"] = np.ascontiguousarray(dnw[:, colsel].reshape(32, 128, 16).transpose(1, 0, 2))
        hc = np.zeros((8, 128, DCW), f32)
        for a in range(8):
            h = 8 * g + a
            for k, o in enumerate((0, 4096, 8192)):
                hc[a, :, 4 * k:4 * k + 4] = cw[:, o + h * 128:o + (h + 1) * 128].T
            hc[a, :, 12] = alog[h]; hc[a, :, 13] = dtb[h]; hc[a, :, 14] = nw
        m["hconst_dn"] = hc
        m["wo_dn"] = np.ascontiguousarray(dno[8 * g * 128:(8 * g + 8) * 128].reshape(8, 128, 4096))
        grp[g] = m
    in_maps = []
    for c in range(NCORE):
        xc = xf[c * TOK_PER_CORE:(c + 1) * TOK_PER_CORE]
        m = {"x": np.ascontiguousarray(xc.reshape(2, 1024, KC, 128).transpose(2, 0, 3, 1).reshape(KC * 2 * 128, 1024))}
        m.update(common); m.update(ffn_sh[c % 4]); m.update(grp[c % 4])
        in_maps.append(m)
    return in_maps
```
